# Optimizing a Trainium2 kernel written in Bass

```python
import math
import jax, jax.numpy as jnp
from jax import lax
import numpy as np

D_MODEL = 1024
BATCH = 4
SEQ = 4096
DEPTH = 2
DEC_BATCH = 128
DEC_SEQ = 4
PAST_LEN = 2048
PAGE_SIZE = 128

N_A_LAYERS = DEPTH // 2
N_B_LAYERS = DEPTH - N_A_LAYERS
D_FF = 2816
D_RNN = D_MODEL
N_RG_BLOCKS = 8
RG_BLOCK = D_RNN // N_RG_BLOCKS
CONV_W = 4
RG_C = 8.0
N_HEADS = 8
HEAD_DIM = D_MODEL // (2 * N_HEADS)
V_DIM = 2 * HEAD_DIM
Q_DIM = N_HEADS * 2 * HEAD_DIM
K_DIM = N_HEADS * 2 * HEAD_DIM
KV_DIM = K_DIM + N_HEADS * V_DIM
PLE_DIM = 256
ROPE_THETA = 10000.0
Q_BLOCK = 128
EPS = 1e-6

kernel_name = "hawk_yoco_diff_attention_step"


def rmsnorm(x, g):
    xf = x.astype(jnp.float32)
    y = xf * lax.rsqrt(jnp.mean(xf * xf, axis=-1, keepdims=True) + EPS) * g.astype(jnp.float32)
    return y.astype(x.dtype)


def swiglu(x, w_gu, w_down):
    g, u = jnp.split(x @ w_gu, 2, axis=-1)
    return (jax.nn.silu(g) * u) @ w_down


def rope(x, pos):
    half = HEAD_DIM // 2
    inv = jnp.power(ROPE_THETA, -jnp.arange(half, dtype=jnp.float32) * 2.0 / HEAD_DIM)
    ang = pos.astype(jnp.float32)[:, None] * inv[None, :]
    cos = jnp.cos(ang)[None, :, None, None, :]
    sin = jnp.sin(ang)[None, :, None, None, :]
    xf = x.astype(jnp.float32)
    x1, x2 = xf[..., :half], xf[..., half:]
    return jnp.concatenate([x1 * cos - x2 * sin, x2 * cos + x1 * sin], axis=-1).astype(x.dtype)


def _lin_combine(e1, e2):
    a1, b1 = e1
    a2, b2 = e2
    return a1 * a2, a2 * b1 + b2


def recurrent_block(xn, conv_prev, h0, w_in, conv_w, conv_b, w_a, b_a, w_i, b_i, lam, w_out):
    B, S = xn.shape[:2]
    gate, u = jnp.split(xn @ w_in, 2, axis=-1)
    u_ext = jnp.concatenate([conv_prev.astype(u.dtype), u], axis=1)
    conv = conv_b + u_ext[:, 0:S] * conv_w[0]
    for k in range(1, CONV_W):
        conv = conv + u_ext[:, k:k + S] * conv_w[k]
    new_conv = u_ext[:, -(CONV_W - 1):]
    cb = conv.reshape(B, S, N_RG_BLOCKS, RG_BLOCK)
    r = jax.nn.sigmoid(jnp.einsum('bsnd,nde->bsne', cb, w_a).reshape(B, S, D_RNN) + b_a)
    i = jax.nn.sigmoid(jnp.einsum('bsnd,nde->bsne', cb, w_i).reshape(B, S, D_RNN) + b_i)
    log_a = -RG_C * r.astype(jnp.float32) * jax.nn.softplus(-lam.astype(jnp.float32))
    a = jnp.exp(log_a)
    mult = jnp.sqrt(-jnp.expm1(2.0 * log_a))
    b = mult * (i * conv).astype(jnp.float32)
    b = b.at[:, 0].add(a[:, 0] * h0.astype(jnp.float32))
    _, h = lax.associative_scan(_lin_combine, (a, b), axis=1)
    new_h = h[:, -1].astype(xn.dtype)
    y = (jax.nn.gelu(gate) * h.astype(gate.dtype)) @ w_out
    return y, new_conv, new_h


def shared_kv(x, pos, kv_norm, w_kv):
    B, S = x.shape[:2]
    kv = rmsnorm(x, kv_norm) @ w_kv
    k, v = jnp.split(kv, [K_DIM], axis=-1)
    k = rope(k.reshape(B, S, N_HEADS, 2, HEAD_DIM), pos).reshape(B, S, 2 * N_HEADS, HEAD_DIM)
    v = v.reshape(B, S, N_HEADS, V_DIM)
    return k, v


def prompt_diff_attention(q, k, v):
    B, S = q.shape[:2]
    nblk = S // Q_BLOCK
    scale = HEAD_DIM ** -0.5
    qb = jnp.moveaxis(q.reshape(B, nblk, Q_BLOCK, N_HEADS, 2, HEAD_DIM), 1, 0)
    k5 = k.reshape(B, S, N_HEADS, 2, HEAD_DIM)
    kpos = jnp.arange(S)

    def one_block(args):
        qblk, j = args
        qpos = j * Q_BLOCK + jnp.arange(Q_BLOCK)
        s = jnp.einsum('bqhcd,bkhcd->bhcqk', qblk, k5).astype(jnp.float32) * scale
        s = jnp.where(kpos[None, :] <= qpos[:, None], s, -jnp.inf)
        p = jax.nn.softmax(s, axis=-1).astype(v.dtype)
        return jnp.einsum('bhcqk,bkhv->bqhcv', p, v)

    o = lax.map(one_block, (qb, jnp.arange(nblk)))
    return jnp.moveaxis(o, 0, 1).reshape(B, S, N_HEADS, 2, V_DIM)


def sample_diff_attention(q, k_new, v_new, cache_k, cache_v, page_table):
    Bd, Sq = q.shape[:2]
    scale = HEAD_DIM ** -0.5
    k5 = k_new.reshape(Bd, Sq, N_HEADS, 2, HEAD_DIM)
    s = jnp.einsum('bqhcd,bkhcd->bhcqk', q, k5).astype(jnp.float32) * scale
    causal = jnp.arange(Sq)[None, :] <= jnp.arange(Sq)[:, None]
    s = jnp.where(causal, s, -jnp.inf)
    m = jnp.max(s, axis=-1)
    p = jnp.exp(s - m[..., None])
    l = jnp.sum(p, axis=-1)
    acc = jnp.einsum('bhcqk,bkhv->bhcqv', p, v_new.astype(jnp.float32))

    def page_step(carry, page_ids):
        m, l, acc = carry
        kp = cache_k[page_ids].reshape(Bd, PAGE_SIZE, N_HEADS, 2, HEAD_DIM)
        vp = cache_v[page_ids]
        sp = jnp.einsum('bqhcd,bkhcd->bhcqk', q, kp.astype(q.dtype)).astype(jnp.float32) * scale
        m_new = jnp.maximum(m, jnp.max(sp, axis=-1))
        corr = jnp.exp(m - m_new)
        pp = jnp.exp(sp - m_new[..., None])
        l = l * corr + jnp.sum(pp, axis=-1)
        acc = acc * corr[..., None] + jnp.einsum('bhcqk,bkhv->bhcqv', pp, vp.astype(jnp.float32))
        return (m_new, l, acc), None

    (m, l, acc), _ = lax.scan(page_step, (m, l, acc), page_table.T)
    o = acc / l[..., None]
    return jnp.moveaxis(o, 3, 1).astype(q.dtype)


def diff_output(o, lam_q1, lam_k1, lam_q2, lam_k2, subln, w_o, lambda_init):
    B, S = o.shape[:2]
    lam = (jnp.exp(jnp.sum(lam_q1.astype(jnp.float32) * lam_k1.astype(jnp.float32)))
           - jnp.exp(jnp.sum(lam_q2.astype(jnp.float32) * lam_k2.astype(jnp.float32)))
           + lambda_init)
    d = o[..., 0, :].astype(jnp.float32) - lam * o[..., 1, :].astype(jnp.float32)
    d = rmsnorm(d, subln) * (1.0 - lambda_init)
    return d.reshape(B, S, N_HEADS * V_DIM).astype(o.dtype) @ w_o


def layer_stack(x, p, pos, conv_prev, h_prev, attend, W):
    B, S = x.shape[:2]
    new_conv, new_h = [], []
    k = v = None
    for i in range(DEPTH):
        x = x + 0.5 * swiglu(rmsnorm(x, W['ffn1_norm'][i]), W['ffn1_w_gu'][i], W['ffn1_w_down'][i])
        xn = rmsnorm(x, W['mix_norm'][i])
        if i < N_A_LAYERS:
            y, c, h = recurrent_block(xn, conv_prev[i], h_prev[i], W['rg_w_in'][i], W['rg_conv_w'][i],
                                      W['rg_conv_b'][i], W['rg_w_a'][i], W['rg_b_a'][i], W['rg_w_i'][i],
                                      W['rg_b_i'][i], W['rg_lambda'][i], W['rg_w_out'][i])
            new_conv.append(c)
            new_h.append(h)
        else:
            j = i - N_A_LAYERS
            q = rope((xn @ W['attn_w_q'][j]).reshape(B, S, N_HEADS, 2, HEAD_DIM), pos)
            o = attend(q, k, v)
            lambda_init = 0.8 - 0.6 * math.exp(-0.3 * i)
            y = diff_output(o, W['lambda_q1'][j], W['lambda_k1'][j], W['lambda_q2'][j], W['lambda_k2'][j],
                            W['attn_subln'][j], W['attn_w_o'][j], lambda_init)
        x = x + y
        x = x + 0.5 * swiglu(rmsnorm(x, W['ffn2_norm'][i]), W['ffn2_w_gu'][i], W['ffn2_w_down'][i])
        gate = jax.nn.sigmoid(rmsnorm(x, W['ple_norm'][i]) @ W['ple_w_gate'][i])
        x = x + gate * (p[i] @ W['ple_w_proj'][i])
        if i == N_A_LAYERS - 1:
            k, v = shared_kv(x, pos, W['kv_norm'], W['w_kv'])
    y = rmsnorm(x, W['final_norm'])
    return y, k, v, jnp.stack(new_conv), jnp.stack(new_h)


def setup_inputs(seed: int = 0) -> dict:
    key = jax.random.key(seed)
    ks = iter(jax.random.split(key, 48))
    f32 = jnp.float32

    def nrm(shape, scale):
        return jax.random.normal(next(ks), shape, f32) * scale

    def gain(shape):
        return 1.0 + nrm(shape, 0.01)

    n_pages = PAST_LEN // PAGE_SIZE
    n_used = DEC_BATCH * n_pages
    n_pool = n_used + max(1, n_used // 4)
    page_table = jax.random.permutation(next(ks), n_pool)[:n_used].reshape(DEC_BATCH, n_pages).astype(jnp.int32)

    u = jax.random.uniform(next(ks), (N_A_LAYERS, D_RNN), f32, 0.9, 0.999)
    a0 = jnp.power(u, 1.0 / RG_C)
    rg_lambda = jnp.log(a0) - jnp.log1p(-a0)

    return {
        'x_prompt': nrm((BATCH, SEQ, D_MODEL), 1.0),
        'x_sample': nrm((DEC_BATCH, DEC_SEQ, D_MODEL), 1.0),
        'p_prompt': nrm((DEPTH, BATCH, SEQ, PLE_DIM), 1.0),
        'p_sample': nrm((DEPTH, DEC_BATCH, DEC_SEQ, PLE_DIM), 1.0),
        'cache_k': nrm((n_pool, PAGE_SIZE, 2 * N_HEADS, HEAD_DIM), 1.0),
        'cache_v': nrm((n_pool, PAGE_SIZE, N_HEADS, V_DIM), 1.0),
        'page_table': page_table,
        'state_conv': nrm((N_A_LAYERS, DEC_BATCH, CONV_W - 1, D_RNN), 1.0),
        'state_rglru': nrm((N_A_LAYERS, DEC_BATCH, D_RNN), 0.5),
        'ffn1_norm': gain((DEPTH, D_MODEL)),
        'ffn1_w_gu': nrm((DEPTH, D_MODEL, 2 * D_FF), D_MODEL ** -0.5),
        'ffn1_w_down': nrm((DEPTH, D_FF, D_MODEL), D_FF ** -0.5),
        'mix_norm': gain((DEPTH, D_MODEL)),
        'rg_w_in': nrm((N_A_LAYERS, D_MODEL, 2 * D_RNN), D_MODEL ** -0.5),
        'rg_conv_w': nrm((N_A_LAYERS, CONV_W, D_RNN), CONV_W ** -0.5),
        'rg_conv_b': nrm((N_A_LAYERS, D_RNN), 0.01),
        'rg_w_a': nrm((N_A_LAYERS, N_RG_BLOCKS, RG_BLOCK, RG_BLOCK), RG_BLOCK ** -0.5),
        'rg_b_a': nrm((N_A_LAYERS, D_RNN), 0.01),
        'rg_w_i': nrm((N_A_LAYERS, N_RG_BLOCKS, RG_BLOCK, RG_BLOCK), RG_BLOCK ** -0.5),
        'rg_b_i': nrm((N_A_LAYERS, D_RNN), 0.01),
        'rg_lambda': rg_lambda,
        'rg_w_out': nrm((N_A_LAYERS, D_RNN, D_MODEL), D_RNN ** -0.5),
        'kv_norm': gain((D_MODEL,)),
        'w_kv': nrm((D_MODEL, KV_DIM), D_MODEL ** -0.5),
        'attn_w_q': nrm((N_B_LAYERS, D_MODEL, Q_DIM), D_MODEL ** -0.5),
        'lambda_q1': nrm((N_B_LAYERS, HEAD_DIM), 0.1),
        'lambda_k1': nrm((N_B_LAYERS, HEAD_DIM), 0.1),
        'lambda_q2': nrm((N_B_LAYERS, HEAD_DIM), 0.1),
        'lambda_k2': nrm((N_B_LAYERS, HEAD_DIM), 0.1),
        'attn_subln': gain((N_B_LAYERS, V_DIM)),
        'attn_w_o': nrm((N_B_LAYERS, N_HEADS * V_DIM, D_MODEL), (N_HEADS * V_DIM) ** -0.5),
        'ffn2_norm': gain((DEPTH, D_MODEL)),
        'ffn2_w_gu': nrm((DEPTH, D_MODEL, 2 * D_FF), D_MODEL ** -0.5),
        'ffn2_w_down': nrm((DEPTH, D_FF, D_MODEL), D_FF ** -0.5),
        'ple_norm': gain((DEPTH, D_MODEL)),
        'ple_w_gate': nrm((DEPTH, D_MODEL, D_MODEL), D_MODEL ** -0.5),
        'ple_w_proj': nrm((DEPTH, PLE_DIM, D_MODEL), PLE_DIM ** -0.5),
        'final_norm': gain((D_MODEL,)),
    }


def reference(x_prompt, x_sample, p_prompt, p_sample, cache_k, cache_v, page_table, state_conv, state_rglru,
              ffn1_norm, ffn1_w_gu, ffn1_w_down, mix_norm, rg_w_in, rg_conv_w, rg_conv_b, rg_w_a, rg_b_a,
              rg_w_i, rg_b_i, rg_lambda, rg_w_out, kv_norm, w_kv, attn_w_q, lambda_q1, lambda_k1, lambda_q2,
              lambda_k2, attn_subln, attn_w_o, ffn2_norm, ffn2_w_gu, ffn2_w_down, ple_norm, ple_w_gate,
              ple_w_proj, final_norm):
    W = {
        'ffn1_norm': ffn1_norm, 'ffn1_w_gu': ffn1_w_gu, 'ffn1_w_down': ffn1_w_down, 'mix_norm': mix_norm,
        'rg_w_in': rg_w_in, 'rg_conv_w': rg_conv_w, 'rg_conv_b': rg_conv_b, 'rg_w_a': rg_w_a,
        'rg_b_a': rg_b_a, 'rg_w_i': rg_w_i, 'rg_b_i': rg_b_i, 'rg_lambda': rg_lambda, 'rg_w_out': rg_w_out,
        'kv_norm': kv_norm, 'w_kv': w_kv, 'attn_w_q': attn_w_q, 'lambda_q1': lambda_q1,
        'lambda_k1': lambda_k1, 'lambda_q2': lambda_q2, 'lambda_k2': lambda_k2, 'attn_subln': attn_subln,
        'attn_w_o': attn_w_o, 'ffn2_norm': ffn2_norm, 'ffn2_w_gu': ffn2_w_gu, 'ffn2_w_down': ffn2_w_down,
        'ple_norm': ple_norm, 'ple_w_gate': ple_w_gate, 'ple_w_proj': ple_w_proj, 'final_norm': final_norm,
    }
    B, S = x_prompt.shape[:2]
    pos_prompt = jnp.arange(S, dtype=jnp.float32)
    conv0 = jnp.zeros((N_A_LAYERS, B, CONV_W - 1, D_RNN), x_prompt.dtype)
    h0 = jnp.zeros((N_A_LAYERS, B, D_RNN), x_prompt.dtype)
    y_prompt, new_k_prompt, new_v_prompt, new_conv_prompt, new_h_prompt = layer_stack(
        x_prompt, p_prompt, pos_prompt, conv0, h0, prompt_diff_attention, W)

    Sd = x_sample.shape[1]
    pos_sample = (PAST_LEN + jnp.arange(Sd)).astype(jnp.float32)

    def attend_sample(q, k, v):
        return sample_diff_attention(q, k, v, cache_k, cache_v, page_table)

    y_sample, new_k_sample, new_v_sample, new_conv_sample, new_h_sample = layer_stack(
        x_sample, p_sample, pos_sample, state_conv, state_rglru, attend_sample, W)

    return (y_prompt, y_sample, new_k_prompt, new_v_prompt, new_k_sample, new_v_sample,
            new_conv_prompt, new_h_prompt, new_conv_sample, new_h_sample)
```

```python
import contextlib
import math
import os
import numpy as np
import concourse.bass as bass
import concourse.mybir as mybir
from concourse.bass_utils import run_bass_kernel_spmd

F32 = mybir.dt.float32
BF16 = mybir.dt.bfloat16
I32 = mybir.dt.int32
AF = mybir.ActivationFunctionType
ALU = mybir.AluOpType
AX = mybir.AxisListType

ENGS = ("tensor", "vector", "scalar", "gpsimd", "sync")


class Buf:
    __slots__ = ("name", "w", "rs", "excl", "xacc")

    def __init__(self, name, rs=None, excl=False):
        self.name = name
        self.w = None
        self.rs = list(rs) if rs else []
        self.excl = excl
        self.xacc = {}


class Op:
    __slots__ = ("eng", "idx", "emit", "deps", "needs_inc", "count", "dma")

    def __init__(self, eng, idx, emit):
        self.eng = eng
        self.idx = idx
        self.emit = emit
        self.deps = []
        self.needs_inc = False
        self.count = 0
        self.dma = None


class Prog:
    def __init__(self, nc):
        self.nc = nc
        self.q = {e: [] for e in ENGS}
        self.dma_counts = {}
        self.stack = contextlib.ExitStack()
        self.nbuf = 0
        self.phase_bufs = []

    def buf(self, name=None):
        self.nbuf += 1
        return Buf(name or f"b{self.nbuf}")

    def new_phase(self):
        evs = set()
        for b in self.phase_bufs:
            if b.w is not None:
                evs.add(b.w)
            evs.update(b.rs)
        evs = list(evs)
        self.phase_bufs = []

        def mk(name=None):
            self.nbuf += 1
            b = Buf(name or f"pb{self.nbuf}", rs=evs)
            self.phase_bufs.append(b)
            return b
        return mk

    def sbuf(self, name, shape, dtype):
        return self.stack.enter_context(self.nc.sbuf_tensor("sb_" + name, list(shape), dtype))

    def psum(self, name, shape, dtype=F32):
        return self.stack.enter_context(self.nc.psum_tensor("ps_" + name, list(shape), dtype))

    def op(self, eng, emit, reads=(), writes=(), dma=None):
        q = self.q[eng]
        o = Op(eng, len(q), emit)
        deps = set()
        for b in reads:
            if b.w is not None:
                deps.add(b.w)
        for b in writes:
            w = b.w
            if w is not None:
                if dma is None and w[0] == "E" and w[1] == eng:
                    pass
                elif dma is not None and w[0] == "D" and w[1] == dma:
                    pass
                else:
                    deps.add(w)
            for r in b.rs:
                if dma is None and r[0] == "E" and r[1] == eng:
                    continue
                deps.add(r)
        for b in list(reads) + list(writes):
            if b.excl:
                for e2, ev2 in b.xacc.items():
                    if e2 != eng:
                        deps.add(ev2)
        if dma is not None:
            c = self.dma_counts.get(dma, 0) + 16
            self.dma_counts[dma] = c
            o.dma = (dma, c)
            ev = ("D", dma, c)
        else:
            ev = ("E", eng, o.idx)
        best = {}
        for d in deps:
            k = (d[0], d[1])
            if k not in best or d[2] > best[k][2]:
                best[k] = d
        deps = list(best.values())
        for d in deps:
            if d[0] == "E":
                self.q[d[1]][d[2]].needs_inc = True
        o.deps = sorted(deps, key=lambda d: (d[0], str(d[1]), d[2]))
        for b in reads:
            b.rs = [r for r in b.rs if not (r[0] == ev[0] and r[1] == ev[1])]
            b.rs.append(ev)
        for b in writes:
            b.w = ev
            b.rs = []
        for b in list(reads) + list(writes):
            if b.excl:
                b.xacc[eng] = ev
        q.append(o)
        return o

    def emit_all(self, final_wait_engine="sync"):
        nc = self.nc
        SEG = 30000
        nseg = {}
        for e in ENGS:
            c = 0
            for o in self.q[e]:
                if o.needs_inc and o.dma is None:
                    c += 1
                    o.count = c
            nseg[e] = (c + SEG - 1) // SEG + 1
        esem = {e: [self.stack.enter_context(nc.semaphore(f"es_{e}{i}")) for i in range(nseg[e])] for e in ENGS}
        dsem = {n: self.stack.enter_context(nc.semaphore(f"ds_{n}")) for n in self.dma_counts}
        engobj = {"tensor": nc.tensor, "vector": nc.vector, "scalar": nc.scalar,
                  "gpsimd": nc.gpsimd, "sync": nc.sync}
        prog = self

        def run_engine(e):
            eng = engobj[e]
            waited = {}
            for o in prog.q[e]:
                for d in o.deps:
                    if d[0] == "E":
                        cnt_ = prog.q[d[1]][d[2]].count
                        seg_ = (cnt_ - 1) // SEG
                        sem = esem[d[1]][seg_]
                        val = cnt_ - seg_ * SEG
                        key = ("E", d[1], seg_)
                    else:
                        sem = dsem[d[1]]
                        val = d[2]
                        key = ("D", d[1])
                    if waited.get(key, 0) >= val:
                        continue
                    waited[key] = val
                    eng.wait_ge(sem, val)
                ins = o.emit()
                if o.dma is not None:
                    ins.then_inc(dsem[o.dma[0]], 16)
                elif o.needs_inc:
                    ins.then_inc(esem[e][(o.count - 1) // SEG], 1)
            if e == final_wait_engine:
                for name, c in prog.dma_counts.items():
                    if waited.get(("D", name), 0) < c:
                        eng.wait_ge(dsem[name], c)
                for e2 in ENGS:
                    if e2 == e:
                        continue
                    last = 0
                    for o in prog.q[e2]:
                        if o.count > last:
                            last = o.count
                    if last > 0:
                        seg_ = (last - 1) // SEG
                        if last - seg_ * SEG > waited.get(("E", e2, seg_), 0):
                            eng.wait_ge(esem[e2][seg_], last - seg_ * SEG)

        with nc.Block() as block:
            @block.tensor
            def _(t):
                run_engine("tensor")

            @block.vector
            def _(v):
                run_engine("vector")

            @block.scalar
            def _(s):
                run_engine("scalar")

            @block.gpsimd
            def _(g):
                run_engine("gpsimd")

            @block.sync
            def _(s):
                run_engine("sync")

    def close(self):
        self.stack.close()


NCORES = int(os.environ.get("MK_CORES", "4"))
NPS = 4 // NCORES
NSS = 128 // NCORES
NSMP = NSS * 4
D = 1024
NCH = 8
DFF = 2816
NFF = 22
SEQ = 4096
GP = 1024
TMAX = GP
EPS = 1e-6
NEG = -30000.0
LAMBDA_INIT = 0.8 - 0.6 * math.exp(-0.3 * 1)
SCALE = 0.125
ARENA_W = 9728
FPARTS = [(0, 3), (3, 3), (6, 3), (9, 3), (12, 3), (15, 3), (18, 2), (20, 2)]
WSLOT = 9216

V_FFN1N0, V_MIXN0, V_FFN2N0, V_PLEN0, V_KVN = 0, 1, 2, 3, 4
V_CW0, V_CB, V_BA, V_BI, V_LAM = 5, 9, 10, 11, 12
V_FFN1N1, V_MIXN1, V_FFN2N1, V_PLEN1 = 13, 14, 15, 16


class Grp:
    def __init__(self, kind, T):
        self.kind = kind
        self.T = T
        self.tiles = [(o, min(512, T - o)) for o in range(0, T, 512)]
        ss = 64 if kind == "s" else 128
        self.subs = [(o, min(ss, T - o)) for o in range(0, T, ss)]


class _Stop(Exception):
    pass


def build_program(npool):
    nc = bass.Bass("TRN2", target_bir_lowering=False)
    P = Prog(nc)
    stop_at = int(os.environ.get("MK_STOP", "1000000"))
    stage_ctr = [0]

    def stage(name):
        stage_ctr[0] += 1
        if stage_ctr[0] == stop_at:
            print("MK_STOP at stage", stage_ctr[0], name, flush=True)
            raise _Stop()

    def din(name, shape, dt=F32):
        return nc.dram_tensor(name, list(shape), dt, kind="ExternalInput").ap()

    def dout(name, shape, dt=F32):
        return nc.dram_tensor(name, list(shape), dt, kind="ExternalOutput").ap()

    def dscr(name, shape, dt):
        return nc.dram_tensor(name, list(shape), dt, kind="Internal").ap()

    xp = din("xp", [NPS * SEQ, D])
    xs = din("xs", [NSMP, D])
    pp0 = din("pp0", [NPS * SEQ, 256])
    pp1 = din("pp1", [NPS * SEQ, 256])
    ps0 = din("ps0", [NSMP, 256])
    ps1 = din("ps1", [NSMP, 256])
    ck = din("ck", [npool * 128, D])
    cv = din("cv", [npool * 128, D])
    pt = din("pt", [1, NSS * 16], I32)
    sconv = din("sconv", [NSS * 3, D])
    sh = din("sh", [NSS, D])
    w_ffn1_norm = din("ffn1_norm", [2, D])
    w_ffn1_gu = din("ffn1_w_gu", [2, D, 2 * DFF])
    w_ffn1_dn = din("ffn1_w_down", [2, DFF, D])
    w_mix_norm = din("mix_norm", [2, D])
    w_rg_in = din("rg_w_in", [1, D, 2 * D])
    w_rg_conv_w = din("rg_conv_w", [1, 4, D])
    w_rg_conv_b = din("rg_conv_b", [1, D])
    w_rg_a = din("rg_w_a", [1, 8, 128, 128])
    w_rg_ba = din("rg_b_a", [1, D])
    w_rg_i = din("rg_w_i", [1, 8, 128, 128])
    w_rg_bi = din("rg_b_i", [1, D])
    w_rg_lam = din("rg_lambda", [1, D])
    w_rg_out = din("rg_w_out", [1, D, D])
    w_kv_norm = din("kv_norm", [D])
    w_kv = din("w_kv", [D, 2 * D])
    w_q = din("attn_w_q", [1, D, D])
    w_lq1 = din("lambda_q1", [1, 64])
    w_lk1 = din("lambda_k1", [1, 64])
    w_lq2 = din("lambda_q2", [1, 64])
    w_lk2 = din("lambda_k2", [1, 64])
    w_subln = din("attn_subln", [1, 128])
    w_o = din("attn_w_o", [1, D, D])
    w_ffn2_norm = din("ffn2_norm", [2, D])
    w_ffn2_gu = din("ffn2_w_gu", [2, D, 2 * DFF])
    w_ffn2_dn = din("ffn2_w_down", [2, DFF, D])
    w_ple_norm = din("ple_norm", [2, D])
    w_ple_gate = din("ple_w_gate", [2, D, D])
    w_ple_proj = din("ple_w_proj", [2, 256, D])
    w_final_norm = din("final_norm", [D])
    c_ropek_c = din("ropek_c", [128, 32, 32])
    c_ropek_s = din("ropek_s", [128, 32, 32])
    c_ropes_c = din("ropes_c", [128, 32])
    c_ropes_s = din("ropes_s", [128, 32])
    c_tri = din("tri", [128, 128])
    c_smask = din("smask", [64, 16, 64])
    c_hmask = din("hmask", [64, 8])

    y_p = dout("y_p", [NPS * SEQ, D])
    y_s = dout("y_s", [NSMP, D])
    nk_p = dout("nk_p", [NPS * SEQ, D])
    nv_p = dout("nv_p", [NPS * SEQ, D])
    nk_s = dout("nk_s", [NSMP, D])
    nv_s = dout("nv_s", [NSMP, D])
    nconv_p = dout("nconv_p", [NPS * 3, D])
    nh_p = dout("nh_p", [NPS, D])
    nconv_s = dout("nconv_s", [NSS * 3, D])
    nh_s = dout("nh_s", [NSS, D])

    ktscr = dscr("ktscr", [128, 8, SEQ], BF16)
    vscr = dscr("vscr", [128, 32, 8 * 129], BF16)
    x1scr = dscr("x1scr", [128, 8, SEQ], F32)

    x = P.sbuf("x", [128, NCH, TMAX], F32)
    xn = P.sbuf("xn", [128, NCH, TMAX], BF16)
    mixo = P.sbuf("mixo", [128, NCH, TMAX], BF16)
    wsl = [P.sbuf(f"wsl{i}", [128, WSLOT], BF16) for i in range(2)]
    wg = P.sbuf("wg", [128, 2, 8, 128], BF16)
    vecs = P.sbuf("vecs", [128, 136], F32)
    vrows = P.sbuf("vrows", [128, 128], F32)
    vrows2 = P.sbuf("vrows2", [8, 128], F32)
    identf = P.sbuf("identf", [128, 128], F32)
    identb = P.sbuf("identb", [128, 128], BF16)
    onesb = P.sbuf("onesb", [128, 128], BF16)
    sq = P.sbuf("sq", [128, NCH, 512], BF16)
    tf32 = [P.sbuf(f"tf32_{i}", [128, 512], F32) for i in range(2)]
    sg = [P.sbuf(f"sg{i}", [128, 512], BF16) for i in range(2)]
    hb = [P.sbuf(f"hb{i}", [128, 3, 512], BF16) for i in range(2)]
    pT = P.sbuf("pT", [128, 2, TMAX], BF16)
    stg = [P.sbuf(f"stg{i}", [128, D], F32) for i in range(2)]
    spneg8 = P.sbuf("spneg8", [128, 8], F32)
    hcarry = P.sbuf("hcarry", [128, 8], F32)
    ucarry = P.sbuf("ucarry", [128, 8, 3], F32)
    subln_b = P.sbuf("subln_b", [128, 128], F32)
    subln_p = P.sbuf("subln_p", [128, 1], F32)
    neglam = P.sbuf("neglam", [128, 1], F32)
    lamt = P.sbuf("lamt", [128, 4, 64], F32)
    lams = P.sbuf("lams", [128, 4], F32)
    ropes_c = P.sbuf("ropes_c", [128, 32], F32)
    ropes_s = P.sbuf("ropes_s", [128, 32], F32)
    tri = P.sbuf("tri", [128, 128], BF16)
    hmask = P.sbuf("hmask", [64, 8], F32)
    idxall = P.sbuf("idxall", [128, NSS * 16], I32)
    iota_p = P.sbuf("iota_p", [128, 1], F32)
    KsT = P.sbuf("KsT", [128, 8, NSMP], BF16)
    Vs_bf = P.sbuf("Vs_bf", [64, NSMP // 64, D], BF16)
    small = P.sbuf("small", [128, 16], F32)
    kvpg = [P.sbuf(f"kvpg{i}", [128, D], BF16) for i in range(10)]
    arena = P.sbuf("arena", [128, ARENA_W], F32)
    pall = P.psum("pall", [128, 8, 512], F32)

    def V(name, r, w, **kw):
        P.op("vector", lambda: getattr(nc.vector, name)(**kw), r, w)

    def A(name, r, w, **kw):
        P.op("scalar", lambda: getattr(nc.scalar, name)(**kw), r, w)

    def T(name, r, w, **kw):
        P.op("tensor", lambda: getattr(nc.tensor, name)(**kw), r, w)

    def G(name, r, w, **kw):
        P.op("gpsimd", lambda: getattr(nc.gpsimd, name)(**kw), r, w)

    def DS(sem, r, w, **kw):
        P.op("sync", lambda: nc.sync.dma_start(**kw), r, w, dma=sem)

    def DG(sem, r, w, **kw):
        P.op("gpsimd", lambda: nc.gpsimd.dma_start(**kw), r, w, dma=sem)

    cnt = {}

    def rr(name, n=2):
        v = cnt.get(name, 0)
        cnt[name] = v + 1
        return v % n

    class Carve:
        def __init__(self):
            self.off = 0

        def f32(self, words):
            a = arena[:, self.off:self.off + words]
            self.off += words
            assert self.off <= ARENA_W, self.off
            return a

        def bf16(self, elems):
            words = (elems + 1) // 2
            a = arena[:, self.off:self.off + words].bitcast(BF16)
            self.off += words
            assert self.off <= ARENA_W, self.off
            return a

    pbk = [P.buf(f"bank{i}") for i in range(8)]
    for b_ in pbk:
        b_.excl = True
    xb = [P.buf(f"x{j}") for j in range(8)]
    xnb = [P.buf(f"xn{j}") for j in range(8)]
    mob = [P.buf(f"mo{j}") for j in range(8)]
    wb = [P.buf(f"w{i}") for i in range(2)]
    wgb = P.buf("wg")
    cb = P.buf("const")
    sqb = P.buf("sq")
    tfb = [P.buf(f"tf{i}") for i in range(2)]
    sgb = [P.buf(f"sg{i}") for i in range(2)]
    hbb = [P.buf(f"hb{i}") for i in range(2)]
    pTb = P.buf("pT")
    stgb = [P.buf(f"stg{i}") for i in range(2)]
    hcb = P.buf("hcarry")
    ucb = P.buf("ucarry")
    ksb = P.buf("KsT")
    vsb = P.buf("Vs")
    ktb = [P.buf(f"kt{j}") for j in range(32)]
    x1b = [P.buf(f"x1_{j}") for j in range(4)]

    def sub_ids(off, n):
        return list(range(off // 128, (off + n + 127) // 128))

    def bl(lst, off, n):
        return [lst[j] for j in sub_ids(off, n)]

    def vcol(vi, n):
        return vecs[:, vi * 8 + n: vi * 8 + n + 1]

    def setup():
        def mk_ident():
            nc.gpsimd.memset(identf[:], 0.0)
            return nc.gpsimd.affine_select(out=identf[:], in_=identf[:], pattern=[[-1, 128]],
                                           compare_op=ALU.not_equal, fill=1.0, base=0, channel_multiplier=1)
        P.op("gpsimd", mk_ident, (), [cb])
        G("iota", [], [cb], out=iota_p[:], pattern=[[0, 1]], base=0, channel_multiplier=1,
          allow_small_or_imprecise_dtypes=True)
        V("tensor_copy", [cb], [cb], out=identb[:], in_=identf[:])
        V("memset", [], [cb], ap=onesb[:], constant=1.0)
        V("memset", [], [cb], ap=vrows[:], constant=0.0)
        vlist = [w_ffn1_norm[0], w_mix_norm[0], w_ffn2_norm[0], w_ple_norm[0], w_kv_norm,
                 w_rg_conv_w[0, 0], w_rg_conv_w[0, 1], w_rg_conv_w[0, 2], w_rg_conv_w[0, 3],
                 w_rg_conv_b[0], w_rg_ba[0], w_rg_bi[0], w_rg_lam[0],
                 w_ffn1_norm[1], w_mix_norm[1], w_ffn2_norm[1]]
        for i, v in enumerate(vlist):
            DS("const", [], [cb], out=vrows[i * 8:(i + 1) * 8, :], in_=v.rearrange("(n p) -> n p", p=128))
        DS("const", [], [cb], out=vrows2[:], in_=w_ple_norm[1].rearrange("(n p) -> n p", p=128))
        T("transpose", [cb], [pbk[0]], out=pall[:, 0, 0:128], in_=vrows[:], identity=identf[:])
        T("transpose", [cb], [pbk[0]], out=pall[:, 0, 128:136], in_=vrows2[:], identity=identf[0:8, 0:8])
        V("tensor_copy", [pbk[0]], [cb], out=vecs[:], in_=pall[:, 0, 0:136])
        lamv = vecs[:, V_LAM * 8:V_LAM * 8 + 8]
        A("activation", [cb], [cb], out=spneg8[:], in_=lamv, func=AF.Exp, scale=-1.0)
        A("activation", [cb], [cb], out=spneg8[:], in_=spneg8[:], func=AF.Ln, bias=1.0)
        V("tensor_scalar", [cb], [cb], out=spneg8[:], in0=spneg8[:], scalar1=-8.0, scalar2=None, op0=ALU.mult)
        DG("wg", [], [wgb], out=wg[:, 0], in_=w_rg_a[0].rearrange("n d e -> d n e"))
        DG("wg", [], [wgb], out=wg[:, 1], in_=w_rg_i[0].rearrange("n d e -> d n e"))
        for i, v in enumerate([w_lq1, w_lk1, w_lq2, w_lk2]):
            DS("const", [], [cb], out=lamt[:, i, :], in_=v.partition_broadcast(128))
        V("tensor_tensor", [cb], [cb], out=lamt[:, 0, :], in0=lamt[:, 0, :], in1=lamt[:, 1, :], op=ALU.mult)
        V("tensor_tensor", [cb], [cb], out=lamt[:, 2, :], in0=lamt[:, 2, :], in1=lamt[:, 3, :], op=ALU.mult)
        V("tensor_reduce", [cb], [cb], out=lams[:, 0:1], in_=lamt[:, 0, :], axis=AX.X, op=ALU.add)
        V("tensor_reduce", [cb], [cb], out=lams[:, 1:2], in_=lamt[:, 2, :], axis=AX.X, op=ALU.add)
        A("activation", [cb], [cb], out=lams[:, 0:2], in_=lams[:, 0:2], func=AF.Exp)
        V("scalar_tensor_tensor", [cb], [cb], out=neglam[:], in0=lams[:, 1:2], scalar=-LAMBDA_INIT, in1=lams[:, 0:1],
          op0=ALU.add, op1=ALU.subtract)
        DS("const", [], [cb], out=subln_b[:], in_=w_subln.partition_broadcast(128))
        V("tensor_scalar", [cb], [cb], out=subln_b[:], in0=subln_b[:], scalar1=1.0 - LAMBDA_INIT, scalar2=None, op0=ALU.mult)
        DS("const", [], [cb], out=small[:, 8:9], in_=w_subln.rearrange("o p -> p o"))
        V("tensor_scalar", [cb], [cb], out=subln_p[:], in0=small[:, 8:9], scalar1=1.0 - LAMBDA_INIT, scalar2=None, op0=ALU.mult)
        DS("const", [], [cb], out=ropes_c[:], in_=c_ropes_c)
        DS("const", [], [cb], out=ropes_s[:], in_=c_ropes_s)
        DG("constg", [], [cb], out=tri[:], in_=c_tri)
        DS("const", [], [cb], out=hmask[:], in_=c_hmask)
        idx_f = arena[:, 0:NSS * 16]
        idxb = P.buf("idxf")
        P.phase_bufs.append(idxb)
        DS("const", [], [cb], out=idxall[:], in_=pt.partition_broadcast(128))
        V("tensor_copy", [cb], [idxb], out=idx_f, in_=idxall[:])
        V("tensor_scalar", [cb, idxb], [idxb], out=idx_f, in0=idx_f, scalar1=128.0, scalar2=iota_p[:, 0:1],
          op0=ALU.mult, op1=ALU.add)
        V("tensor_copy", [idxb], [cb], out=idxall[:], in_=idx_f)

    def load_tm_to_fm(dram_rows, ntok, nfeat, dst3, col, dstbufs):
        s = rr("stg")
        nch = nfeat // 128
        DS(f"stg{s}", [], [stgb[s]], out=stg[s][0:ntok, 0:nfeat], in_=dram_rows)
        for b0 in range(0, nch, 4):
            nb = min(4, nch - b0)
            bk = 6 + rr("tp")
            for c in range(nb):
                T("transpose", [stgb[s], cb], [pbk[bk]], out=pall[:, bk, c * 128:c * 128 + ntok],
                  in_=stg[s][0:ntok, (b0 + c) * 128:(b0 + c + 1) * 128], identity=identf[0:ntok, 0:ntok])
            src = pall[:, bk, 0:nb * 128].rearrange("p (c t) -> p c t", c=nb)[:, :, 0:ntok]
            if rr("evq") == 0:
                A("activation", [pbk[bk]], dstbufs, out=dst3[:, b0:b0 + nb, col:col + ntok], in_=src, func=AF.Copy)
            else:
                V("tensor_copy", [pbk[bk]], dstbufs, out=dst3[:, b0:b0 + nb, col:col + ntok], in_=src)

    def rmsnorm(tiles, vi):
        for (off, n) in tiles:
            A("activation", bl(xb, off, n), [sqb], out=sq[:, :, 0:n], in_=x[:, :, off:off + n], func=AF.Square)
            bk = 6 + rr("nb")
            for k in range(8):
                T("matmul", [sqb, cb], [pbk[bk]], out=pall[:, bk, 0:n], lhsT=onesb[:], rhs=sq[:, k, 0:n],
                  start=(k == 0), stop=(k == 7))
            ri = rr("tf")
            rs = tf32[ri]
            A("activation", [pbk[bk]], [tfb[ri]], out=rs[:, 0:n], in_=pall[:, bk, 0:n], func=AF.Sqrt,
              scale=1.0 / D, bias=EPS)
            V("reciprocal", [tfb[ri]], [tfb[ri]], out=rs[:, 0:n], in_=rs[:, 0:n])
            for k in range(8):
                V("scalar_tensor_tensor", bl(xb, off, n) + [tfb[ri], cb], bl(xnb, off, n),
                  out=xn[:, k, off:off + n], in0=x[:, k, off:off + n], scalar=vcol(vi, k), in1=rs[:, 0:n],
                  op0=ALU.mult, op1=ALU.mult)

    def next_wslot():
        return rr("wslot")

    def ffn(wgu, wdn, tiles, vi):
        rmsnorm(tiles, vi)
        wguv = wgu.rearrange("(kc p) f -> p kc f", p=128)
        wdnv = wdn.rearrange("(fc p) d -> p fc d", p=128)
        pending = [None]

        def make_down(s, wdv, nf, hs, off, n):
            def go():
                for dm in range(8):
                    ab = 4 + rr("acc")
                    for fi in range(nf):
                        T("matmul", [wb[s], hbb[hs]], [pbk[ab]], out=pall[:, ab, 0:n],
                          lhsT=wdv[:, fi, dm * 128:(dm + 1) * 128], rhs=hb[hs][:, fi, 0:n],
                          start=(fi == 0), stop=(fi == nf - 1))
                    V("scalar_tensor_tensor", [pbk[ab]] + bl(xb, off, n), bl(xb, off, n),
                      out=x[:, dm, off:off + n], in0=pall[:, ab, 0:n], scalar=0.5, in1=x[:, dm, off:off + n],
                      op0=ALU.mult, op1=ALU.add)
            return go

        for (f0, nf) in FPARTS:
            s = next_wslot()
            W = wsl[s]
            ncol = nf * 128
            wgv = W[:, 0:8 * ncol].rearrange("p (k f) -> p k f", k=8)
            wuv = W[:, 8 * ncol:16 * ncol].rearrange("p (k f) -> p k f", k=8)
            wdv = W[:, 16 * ncol:16 * ncol + nf * D].rearrange("p (c d) -> p c d", c=nf)
            DG(f"w{s}", [], [wb[s]], out=wgv, in_=wguv[:, :, f0 * 128:f0 * 128 + ncol])
            DG(f"w{s}", [], [wb[s]], out=wuv, in_=wguv[:, :, DFF + f0 * 128:DFF + f0 * 128 + ncol])
            DG(f"w{s}", [], [wb[s]], out=wdv, in_=wdnv[:, f0:f0 + nf, :])
            for (off, n) in tiles:
                hs = rr("hb")
                for fi in range(nf):
                    gb = 0 + rr("gb")
                    ub = 2 + rr("ub")
                    for k in range(8):
                        T("matmul", [wb[s]] + bl(xnb, off, n), [pbk[gb]], out=pall[:, gb, 0:n],
                          lhsT=wgv[:, k, fi * 128:(fi + 1) * 128], rhs=xn[:, k, off:off + n],
                          start=(k == 0), stop=(k == 7))
                    for k in range(8):
                        T("matmul", [wb[s]] + bl(xnb, off, n), [pbk[ub]], out=pall[:, ub, 0:n],
                          lhsT=wuv[:, k, fi * 128:(fi + 1) * 128], rhs=xn[:, k, off:off + n],
                          start=(k == 0), stop=(k == 7))
                    si = rr("sg")
                    A("activation", [pbk[gb]], [sgb[si]], out=sg[si][:, 0:n], in_=pall[:, gb, 0:n], func=AF.Silu)
                    V("tensor_tensor", [sgb[si], pbk[ub]], [hbb[hs]], out=hb[hs][:, fi, 0:n], in0=sg[si][:, 0:n],
                      in1=pall[:, ub, 0:n], op=ALU.mult)
                if pending[0] is not None:
                    pending[0]()
                pending[0] = make_down(s, wdv, nf, hs, off, n)
        pending[0]()

    def dense_add(wslot_s, wv, src, srcbufs, tiles, nk):
        for (off, n) in tiles:
            for dm in range(8):
                ab = 4 + rr("acc")
                for k in range(nk):
                    T("matmul", [wb[wslot_s]] + bl(srcbufs, off, n), [pbk[ab]], out=pall[:, ab, 0:n],
                      lhsT=wv[:, k, dm * 128:(dm + 1) * 128], rhs=src[:, k, off:off + n],
                      start=(k == 0), stop=(k == nk - 1))
                V("tensor_tensor", [pbk[ab]] + bl(xb, off, n), bl(xb, off, n), out=x[:, dm, off:off + n],
                  in0=pall[:, ab, 0:n], in1=x[:, dm, off:off + n], op=ALU.add)

    def load_w_full(dram_w, s, nk):
        wv = wsl[s][:, 0:nk * D].rearrange("p (k d) -> p k d", k=nk)
        DG(f"w{s}", [], [wb[s]], out=wv, in_=dram_w.rearrange("(k p) d -> p k d", p=128))
        return wv

    def ple(layer, grp, prow, vi):
        rmsnorm(grp.tiles, vi)
        for (off, nt) in grp.subs:
            load_tm_to_fm(prow(off, nt), nt, 256, pT, off, [pTb])
        s1 = next_wslot()
        wgv = load_w_full(w_ple_gate[layer], s1, 8)
        s2 = next_wslot()
        wpv = load_w_full(w_ple_proj[layer], s2, 2)
        for (off, n) in grp.tiles:
            for dm in range(8):
                gb = 0 + rr("gb")
                ub = 2 + rr("ub")
                for k in range(8):
                    T("matmul", [wb[s1]] + bl(xnb, off, n), [pbk[gb]], out=pall[:, gb, 0:n],
                      lhsT=wgv[:, k, dm * 128:(dm + 1) * 128], rhs=xn[:, k, off:off + n], start=(k == 0), stop=(k == 7))
                for k in range(2):
                    T("matmul", [wb[s2], pTb], [pbk[ub]], out=pall[:, ub, 0:n],
                      lhsT=wpv[:, k, dm * 128:(dm + 1) * 128], rhs=pT[:, k, off:off + n], start=(k == 0), stop=(k == 1))
                ti = rr("tf")
                A("activation", [pbk[gb]], [tfb[ti]], out=tf32[ti][:, 0:n], in_=pall[:, gb, 0:n], func=AF.Sigmoid)
                V("tensor_tensor", [tfb[ti], pbk[ub]], [tfb[ti]], out=tf32[ti][:, 0:n], in0=tf32[ti][:, 0:n],
                  in1=pall[:, ub, 0:n], op=ALU.mult)
                V("tensor_tensor", [tfb[ti]] + bl(xb, off, n), bl(xb, off, n), out=x[:, dm, off:off + n],
                  in0=tf32[ti][:, 0:n], in1=x[:, dm, off:off + n], op=ALU.add)

    def fm_to_dram(src3, ncols, dram_rows, srcbufs):
        s = rr("stg")
        for half in range(2):
            bk = 6 + rr("tp")
            for c in range(4):
                n = half * 4 + c
                T("transpose", srcbufs + [cb], [pbk[bk]], out=pall[0:ncols, bk, c * 128:(c + 1) * 128],
                  in_=src3[:, n, 0:ncols], identity=identf[:])
            if half == 0:
                A("activation", [pbk[bk]], [stgb[s]], out=stg[s][0:ncols, 0:512], in_=pall[0:ncols, bk, :], func=AF.Copy)
            else:
                V("tensor_copy", [pbk[bk]], [stgb[s]], out=stg[s][0:ncols, 512:1024], in_=pall[0:ncols, bk, :])
        DS(f"stg{s}", [stgb[s]], [], out=dram_rows, in_=stg[s][0:ncols, :])

    def mixer0(grp, si, g):
        T_ = grp.T
        is_s = grp.kind == "s"
        tiles = grp.tiles
        mk = P.new_phase()
        cv_ = Carve()
        TW = T_ + 8
        gg = [cv_.f32(T_) for _ in range(2)]
        ue = [cv_.f32(TW) for _ in range(2)]
        conv = cv_.f32(T_)
        tA = cv_.f32(T_)
        tI = cv_.f32(T_)
        tB = cv_.f32(T_)
        conv_bf = cv_.bf16(T_)
        ggb = [mk("gg0"), mk("gg1")]
        ueb = [mk("ue0"), mk("ue1")]
        convb, tAb, tIb, tBb, cbfb = mk("conv"), mk("tA"), mk("tI"), mk("tB"), mk("cbf")
        if is_s:
            ues = [cv_.f32(NSS * 7).rearrange("p (s k) -> p s k", s=NSS) for _ in range(2)]
            sconvT = cv_.f32(8 * NSS * 3).rearrange("p (n c) -> p n c", n=8)
            shT = cv_.f32(8 * NSS).rearrange("p (n c) -> p n c", n=8)
            nhsT = cv_.f32(8 * NSS).rearrange("p (n c) -> p n c", n=8)
            ncsT = cv_.f32(8 * NSS * 3).rearrange("p (n c) -> p n c", n=8)
            uesb = [mk("ues0"), mk("ues1")]
            stb = mk("stT")
            outb = mk("outT")
            for r0 in range(0, NSS * 3, 96):
                nr = min(96, NSS * 3 - r0)
                load_tm_to_fm(sconv[r0:r0 + nr, :], nr, D, sconvT, r0, [stb])
            for r0 in range(0, NSS, 128):
                nr = min(128, NSS - r0)
                load_tm_to_fm(sh[r0:r0 + nr, :], nr, D, shT, r0, [stb])
        elif g == 0:
            V("memset", [], [hcb], ap=hcarry[:], constant=0.0)
            V("memset", [], [ucb], ap=ucarry[:], constant=0.0)
        slots = {}
        win = w_rg_in[0].rearrange("(kc p) f -> p kc f", p=128)

        def load_in_unit(u):
            s = next_wslot()
            gv = wsl[s][:, 0:4096].rearrange("p (k f) -> p k f", k=8)
            uv = wsl[s][:, 4096:8192].rearrange("p (k f) -> p k f", k=8)
            DG(f"w{s}", [], [wb[s]], out=gv, in_=win[:, :, u * 512:(u + 1) * 512])
            DG(f"w{s}", [], [wb[s]], out=uv, in_=win[:, :, D + u * 512:D + (u + 1) * 512])
            slots[u] = (s, gv, uv)

        load_in_unit(0)
        load_in_unit(1)
        for n in range(8):
            su = n % 2
            s, gv, uv = slots[n // 4]
            c0 = (n % 4) * 128
            for (off, nn) in tiles:
                gb = 0 + rr("gb")
                ub = 2 + rr("ub")
                for k in range(8):
                    T("matmul", [wb[s]] + bl(xnb, off, nn), [pbk[gb]], out=pall[:, gb, 0:nn],
                      lhsT=gv[:, k, c0:c0 + 128], rhs=xn[:, k, off:off + nn], start=(k == 0), stop=(k == 7))
                for k in range(8):
                    T("matmul", [wb[s]] + bl(xnb, off, nn), [pbk[ub]], out=pall[:, ub, 0:nn],
                      lhsT=uv[:, k, c0:c0 + 128], rhs=xn[:, k, off:off + nn], start=(k == 0), stop=(k == 7))
                A("activation", [pbk[gb]], [ggb[su]], out=gg[su][:, off:off + nn], in_=pall[:, gb, 0:nn],
                  func=AF.Gelu_apprx_tanh)
                if not is_s:
                    A("activation", [pbk[ub]], [ueb[su]], out=ue[su][:, 3 + off:3 + off + nn], in_=pall[:, ub, 0:nn],
                      func=AF.Copy)
                else:
                    ns_ = nn // 4
                    s0 = off // 4
                    A("activation", [pbk[ub]], [uesb[su]], out=ues[su][:, s0:s0 + ns_, 3:7],
                      in_=pall[:, ub, 0:nn].rearrange("p (s t) -> p s t", t=4), func=AF.Copy)
            if not is_s:
                V("tensor_copy", [ucb], [ueb[su]], out=ue[su][:, 0:3], in_=ucarry[:, n, :])
                V("tensor_scalar", [ueb[su], cb], [convb], out=conv[:, 0:T_], in0=ue[su][:, 0:T_],
                  scalar1=vcol(V_CW0, n), scalar2=vcol(V_CB, n), op0=ALU.mult, op1=ALU.add)
                for k in range(1, 4):
                    V("scalar_tensor_tensor", [ueb[su], cb, convb], [convb], out=conv[:, 0:T_], in0=ue[su][:, k:k + T_],
                      scalar=vcol(V_CW0 + k, n), in1=conv[:, 0:T_], op0=ALU.mult, op1=ALU.add)
                V("tensor_copy", [ueb[su]], [ucb], out=ucarry[:, n, :], in_=ue[su][:, T_:T_ + 3])
            else:
                cs = conv[:, 0:T_].rearrange("p (s t) -> p s t", t=4)
                V("tensor_copy", [stb], [uesb[su]], out=ues[su][:, :, 0:3],
                  in_=sconvT[:, n, :].rearrange("p (s k) -> p s k", k=3))
                V("tensor_scalar", [uesb[su], cb], [convb], out=cs, in0=ues[su][:, :, 0:4],
                  scalar1=vcol(V_CW0, n), scalar2=vcol(V_CB, n), op0=ALU.mult, op1=ALU.add)
                for k in range(1, 4):
                    V("scalar_tensor_tensor", [uesb[su], cb, convb], [convb], out=cs, in0=ues[su][:, :, k:k + 4],
                      scalar=vcol(V_CW0 + k, n), in1=cs, op0=ALU.mult, op1=ALU.add)
                V("tensor_copy", [uesb[su]], [outb], out=ncsT[:, n, :].rearrange("p (s k) -> p s k", k=3),
                  in_=ues[su][:, :, 4:7])
            A("activation", [convb], [cbfb], out=conv_bf[:, 0:T_], in_=conv[:, 0:T_], func=AF.Copy)
            for (off, nn) in tiles:
                gb = 0 + rr("gb")
                ub = 2 + rr("ub")
                T("matmul", [wgb, cbfb], [pbk[gb]], out=pall[:, gb, 0:nn], lhsT=wg[:, 0, n, :],
                  rhs=conv_bf[:, off:off + nn], start=True, stop=True)
                T("matmul", [wgb, cbfb], [pbk[ub]], out=pall[:, ub, 0:nn], lhsT=wg[:, 1, n, :],
                  rhs=conv_bf[:, off:off + nn], start=True, stop=True)
                A("activation", [pbk[gb], cb], [tAb], out=tA[:, off:off + nn], in_=pall[:, gb, 0:nn], func=AF.Sigmoid,
                  bias=vcol(V_BA, n))
                A("activation", [pbk[ub], cb], [tIb], out=tI[:, off:off + nn], in_=pall[:, ub, 0:nn], func=AF.Sigmoid,
                  bias=vcol(V_BI, n))
            A("activation", [tAb, cb], [tAb], out=tA[:, 0:T_], in_=tA[:, 0:T_], func=AF.Exp, scale=spneg8[:, n:n + 1])
            A("activation", [tAb], [tBb], out=tB[:, 0:T_], in_=tA[:, 0:T_], func=AF.Square)
            A("activation", [tBb], [tBb], out=tB[:, 0:T_], in_=tB[:, 0:T_], func=AF.Sqrt, scale=-1.0, bias=1.0)
            V("tensor_tensor", [tBb, tIb], [tIb], out=tI[:, 0:T_], in0=tB[:, 0:T_], in1=tI[:, 0:T_], op=ALU.mult)
            V("tensor_tensor", [tIb, convb], [tBb], out=tB[:, 0:T_], in0=tI[:, 0:T_], in1=conv[:, 0:T_], op=ALU.mult)
            if not is_s:
                V("tensor_tensor_scan", [tAb, tBb, hcb, convb], [convb], out=conv[:, 0:T_], data0=tA[:, 0:T_],
                  data1=tB[:, 0:T_], initial=hcarry[:, n:n + 1], op0=ALU.mult, op1=ALU.add)
                V("tensor_copy", [convb], [hcb], out=hcarry[:, n:n + 1], in_=conv[:, T_ - 1:T_])
            else:
                hs_ = conv[:, 0:T_].rearrange("p (s t) -> p s t", t=4)
                as_ = tA[:, 0:T_].rearrange("p (s t) -> p s t", t=4)
                bs_ = tB[:, 0:T_].rearrange("p (s t) -> p s t", t=4)
                prev = shT[:, n, :]
                for t in range(4):
                    V("tensor_tensor", [tAb, stb, convb], [convb], out=hs_[:, :, t], in0=as_[:, :, t], in1=prev, op=ALU.mult)
                    V("tensor_tensor", [tBb, convb], [convb], out=hs_[:, :, t], in0=hs_[:, :, t], in1=bs_[:, :, t], op=ALU.add)
                    prev = hs_[:, :, t]
                V("tensor_copy", [convb], [outb], out=nhsT[:, n, :], in_=hs_[:, :, 3])
            V("tensor_tensor", [ggb[su], convb], bl(mob, 0, T_), out=mixo[:, n, 0:T_], in0=gg[su][:, 0:T_],
              in1=conv[:, 0:T_], op=ALU.mult)
        s = next_wslot()
        wov = load_w_full(w_rg_out[0], s, 8)
        dense_add(s, wov, mixo, mob, tiles, 8)
        if not is_s and g == 3:
            fm_to_dram(ucarry[:], 3, nconv_p[si * 3:si * 3 + 3, :], [ucb])
            fm_to_dram(hcarry[:].rearrange("p (n o) -> p n o", o=1), 1, nh_p[si:si + 1, :], [hcb])
        if is_s:
            for r0 in range(0, NSS * 3, 96):
                nr = min(96, NSS * 3 - r0)
                fm_to_dram(ncsT[:, :, r0:r0 + nr], nr, nconv_s[r0:r0 + nr, :], [outb])
            for r0 in range(0, NSS, 128):
                nr = min(128, NSS - r0)
                fm_to_dram(nhsT[:, :, r0:r0 + nr], nr, nh_s[r0:r0 + nr, :], [outb])

    def rope_tm(src_ap, ntok, cos_ap, sin_ap, out4, tmp, srcbufs, outbufs, tmpb):
        sv = src_ap.rearrange("p (h c d) -> p h c d", h=16, c=2)
        cosb = cos_ap.unsqueeze(1).to_broadcast([ntok, 16, 32])
        sinb = sin_ap.unsqueeze(1).to_broadcast([ntok, 16, 32])
        V("tensor_tensor", srcbufs + [cb], outbufs, out=out4[:, :, 0, :], in0=sv[:, :, 0, :], in1=cosb, op=ALU.mult)
        V("tensor_tensor", srcbufs + [cb], [tmpb], out=tmp, in0=sv[:, :, 1, :], in1=sinb, op=ALU.mult)
        V("tensor_tensor", outbufs + [tmpb], outbufs, out=out4[:, :, 0, :], in0=out4[:, :, 0, :], in1=tmp, op=ALU.subtract)
        V("tensor_tensor", srcbufs + [cb], outbufs, out=out4[:, :, 1, :], in0=sv[:, :, 1, :], in1=cosb, op=ALU.mult)
        V("tensor_tensor", srcbufs + [cb, tmpb], [tmpb], out=tmp, in0=sv[:, :, 0, :], in1=sinb, op=ALU.mult)
        V("tensor_tensor", outbufs + [tmpb], outbufs, out=out4[:, :, 1, :], in0=out4[:, :, 1, :], in1=tmp, op=ALU.add)

    def kv_phase(grp, si, g):
        is_s = grp.kind == "s"
        rmsnorm(grp.tiles, V_KVN)
        mk = P.new_phase()
        cv_ = Carve()
        kst = [cv_.f32(D) for _ in range(2)]
        vst = [cv_.f32(D) for _ in range(2)]
        kbf = [cv_.bf16(D) for _ in range(2)]
        ktst = [cv_.bf16(D) for _ in range(2)]
        vbf = [cv_.bf16(8 * 129).rearrange("p (h v) -> p h v", h=8) for _ in range(2)]
        rtc = cv_.f32(256).rearrange("p (t d) -> p t d", t=8)
        rts = cv_.f32(256).rearrange("p (t d) -> p t d", t=8)
        tmpr = [cv_.f32(512).rearrange("p (h d) -> p h d", h=16) for _ in range(2)]
        kstb = [mk(), mk()]
        vstb = [mk(), mk()]
        kbfb = [mk(), mk()]
        ktstb = [mk(), mk()]
        vbfb = [mk(), mk()]
        rtb = mk()
        tmpb = [mk(), mk()]
        if not is_s:
            DS("rt", [], [rtb], out=rtc, in_=c_ropek_c[:, g * 8:(g + 1) * 8, :])
            DS("rt", [], [rtb], out=rts, in_=c_ropek_s[:, g * 8:(g + 1) * 8, :])
            for i in range(2):
                V("memset", [], [vbfb[i]], ap=vbf[i][:, :, 128:129], constant=1.0)
        sK = next_wslot()
        wkv_v = w_kv.rearrange("(k p) f -> p k f", p=128)
        wK = wsl[sK][:, 0:8 * D].rearrange("p (k d) -> p k d", k=8)
        DG(f"w{sK}", [], [wb[sK]], out=wK, in_=wkv_v[:, :, 0:D])
        sV = next_wslot()
        wV = wsl[sV][:, 0:8 * D].rearrange("p (k d) -> p k d", k=8)
        DG(f"w{sV}", [], [wb[sV]], out=wV, in_=wkv_v[:, :, D:2 * D])
        for (off, nt) in grp.subs:
            gt = g * 8 + off // 128
            r = rr("kvr")
            for half in range(2):
                for k in range(8):
                    T("matmul", [wb[sK]] + bl(xnb, off, nt), [pbk[half]], out=pall[0:nt, half, :],
                      lhsT=xn[:, k, off:off + nt], rhs=wK[:, k, half * 512:(half + 1) * 512], start=(k == 0), stop=(k == 7))
            for half in range(2):
                for k in range(8):
                    T("matmul", [wb[sV]] + bl(xnb, off, nt), [pbk[2 + half]], out=pall[0:nt, 2 + half, :],
                      lhsT=xn[:, k, off:off + nt], rhs=wV[:, k, half * 512:(half + 1) * 512], start=(k == 0), stop=(k == 7))
            ksrc = pall[0:nt, 0:2, :].rearrange("p b f -> p (b f)")
            vsrc = pall[0:nt, 2:4, :].rearrange("p b f -> p (b f)")
            if is_s:
                cos_ap, sin_ap = ropes_c[0:nt, :], ropes_s[0:nt, :]
                rbufs = [pbk[0], pbk[1]]
            else:
                cos_ap, sin_ap = rtc[:, off // 128, :], rts[:, off // 128, :]
                rbufs = [pbk[0], pbk[1], rtb]
            out4 = kst[r][0:nt, :].rearrange("p (h c d) -> p h c d", h=16, c=2)
            rope_tm(ksrc, nt, cos_ap, sin_ap, out4, tmpr[r][0:nt], rbufs, [kstb[r]], tmpb[r])
            A("activation", [pbk[2], pbk[3]], [vstb[r]], out=vst[r][0:nt, :], in_=vsrc, func=AF.Copy)
            if is_s:
                DS(f"kst{r}", [kstb[r]], [], out=nk_s[off:off + nt, :], in_=kst[r][0:nt, :])
                DS(f"vst{r}", [vstb[r]], [], out=nv_s[off:off + nt, :], in_=vst[r][0:nt, :])
            else:
                row0 = si * SEQ + gt * 128
                DS(f"kst{r}", [kstb[r]], [], out=nk_p[row0:row0 + 128, :], in_=kst[r][0:nt, :])
                DS(f"vst{r}", [vstb[r]], [], out=nv_p[row0:row0 + 128, :], in_=vst[r][0:nt, :])
            A("activation", [kstb[r]], [kbfb[r]], out=kbf[r][0:nt, :], in_=kst[r][0:nt, :], func=AF.Copy)
            bk = 6 + rr("tp")
            pbv = pall[:, bk, :].bitcast(BF16)
            for j in range(8):
                T("transpose", [kbfb[r], cb], [pbk[bk]], out=pbv[:, j * 128:j * 128 + nt],
                  in_=kbf[r][0:nt, j * 128:(j + 1) * 128], identity=identb[0:nt, 0:nt])
            if is_s:
                V("tensor_copy", [pbk[bk]], [ksb], out=KsT[:, :, off:off + nt],
                  in_=pbv.rearrange("p (j t) -> p j t", j=8)[:, :, 0:nt])
                V("tensor_copy", [vstb[r]], [vsb], out=Vs_bf[0:nt, off // 64, :], in_=vst[r][0:nt, :])
            else:
                V("tensor_copy", [pbk[bk]], [ktstb[r]], out=ktst[r], in_=pbv)
                DS(f"ktst{r}", [ktstb[r]], [ktb[gt]], out=ktscr[:, :, gt * 128:(gt + 1) * 128],
                   in_=ktst[r].rearrange("p (j t) -> p j t", j=8))
                V("tensor_copy", [vstb[r]], [vbfb[r]], out=vbf[r][:, :, 0:128],
                  in_=vst[r].rearrange("p (h v) -> p h v", h=8))
                DS(f"vbf{r}", [vbfb[r]], [ktb[gt]], out=vscr[:, gt, :], in_=vbf[r].rearrange("p h v -> p (h v)"))
        if not is_s:
            DS("x1st", bl(xb, 0, GP), [x1b[g]], out=x1scr[:, :, g * GP:(g + 1) * GP], in_=x[:, :, 0:GP])

    def q_phase(grp, g):
        is_s = grp.kind == "s"
        mk = P.new_phase()
        cv_ = Carve()
        qbf = [cv_.bf16(D) for _ in range(2)]
        tmpr = [cv_.f32(512).rearrange("p (h d) -> p h d", h=16) for _ in range(2)]
        rtc = cv_.f32(256).rearrange("p (t d) -> p t d", t=8)
        rts = cv_.f32(256).rearrange("p (t d) -> p t d", t=8)
        qbfb = [mk(), mk()]
        tmpb = [mk(), mk()]
        rtb = mk()
        if not is_s:
            DS("rt", [], [rtb], out=rtc, in_=c_ropek_c[:, g * 8:(g + 1) * 8, :])
            DS("rt", [], [rtb], out=rts, in_=c_ropek_s[:, g * 8:(g + 1) * 8, :])
        s = next_wslot()
        wq = load_w_full(w_q[0], s, 8)
        for (off, nt) in grp.subs:
            r = rr("qr")
            for half in range(2):
                for k in range(8):
                    T("matmul", [wb[s]] + bl(xnb, off, nt), [pbk[half]], out=pall[0:nt, half, :],
                      lhsT=xn[:, k, off:off + nt], rhs=wq[:, k, half * 512:(half + 1) * 512], start=(k == 0), stop=(k == 7))
            qsrc = pall[0:nt, 0:2, :].rearrange("p b f -> p (b f)")
            if is_s:
                cos_ap, sin_ap = ropes_c[0:nt, :], ropes_s[0:nt, :]
                rbufs = [pbk[0], pbk[1]]
            else:
                cos_ap, sin_ap = rtc[:, off // 128, :], rts[:, off // 128, :]
                rbufs = [pbk[0], pbk[1], rtb]
            out4 = qbf[r][0:nt, :].rearrange("p (h c d) -> p h c d", h=16, c=2)
            rope_tm(qsrc, nt, cos_ap, sin_ap, out4, tmpr[r][0:nt], rbufs, [qbfb[r]], tmpb[r])
            bk = 6 + rr("tp")
            pbv = pall[:, bk, :].bitcast(BF16)
            for j in range(8):
                T("transpose", [qbfb[r], cb], [pbk[bk]], out=pbv[:, j * 128:j * 128 + nt],
                  in_=qbf[r][0:nt, j * 128:(j + 1) * 128], identity=identb[0:nt, 0:nt])
            A("activation", [pbk[bk]], bl(xnb, off, nt), out=xn[:, :, off:off + nt],
              in_=pbv.rearrange("p (j t) -> p j t", j=8)[:, :, 0:nt], func=AF.Copy)

    def attention(g):
        mk = P.new_phase()
        cv_ = Carve()
        nkb = 8 * g + 8
        KTh = [cv_.bf16(SEQ) for _ in range(2)]
        Vh = [cv_.bf16(32 * 129).rearrange("p (b v) -> p b v", b=32) for _ in range(2)]
        PT = [[cv_.bf16(512) for _ in range(2)] for _ in range(2)]
        t1 = cv_.f32(128)
        dd = cv_.f32(128)
        dn = cv_.bf16(128)
        junk = cv_.bf16(128)
        sc = cv_.f32(8)
        kthb = [mk(), mk()]
        vhb = [mk(), mk()]
        ptb = [[mk(), mk()], [mk(), mk()]]
        epb = mk()
        ob = [[pbk[4 + (j * 2 + c) // 3] for c in range(2)] for j in range(4)]

        def oreg(j, c):
            r = j * 2 + c
            return pall[:, 4 + r // 3, (r % 3) * 129:(r % 3) * 129 + 129]

        ocp = sq[:].rearrange("p a b -> p (a b)").bitcast(F32)

        def oreg_sb(j, c):
            r = j * 2 + c
            return ocp[:, (r // 3) * 512 + (r % 3) * 129:(r // 3) * 512 + (r % 3) * 129 + 129]

        for h in range(8):
            s = h % 2
            DS(f"kth{s}", ktb[0:nkb], [kthb[s]], out=KTh[s][:, 0:nkb * 128], in_=ktscr[:, h, 0:nkb * 128])
            DS(f"vh{s}", ktb[0:nkb], [vhb[s]], out=Vh[s][:, 0:nkb, :], in_=vscr[:, 0:nkb, h * 129:(h + 1) * 129])
            for sgi in range(2):
                i0 = 8 * g + 4 * sgi
                cols0 = 4 * sgi * 128
                nkbs = i0 + 4
                opened = set()

                def emit_qk(kb):
                    j0 = max(0, kb - i0)
                    diag = kb - i0 if kb >= i0 else None
                    alt = rr("salt")
                    for c in range(2):
                        bk = 2 * c + alt
                        ps = pall[:, bk, :]
                        pr = slice(64 * c, 64 * c + 64)
                        jr = j0
                        if diag is not None:
                            j = diag
                            T("matmul", [cb], [pbk[bk]], out=ps[:, j * 128:(j + 1) * 128], lhsT=identb[:], rhs=tri[:],
                              start=True, stop=False)
                            T("matmul", [kthb[s]] + bl(xnb, cols0 + j * 128, 128), [pbk[bk]],
                              out=ps[:, j * 128:(j + 1) * 128], lhsT=KTh[s][pr, kb * 128:(kb + 1) * 128],
                              rhs=xn[pr, h, cols0 + j * 128:cols0 + (j + 1) * 128], start=False, stop=True)
                            jr = j0 + 1
                        if jr < 4:
                            T("matmul", [kthb[s]] + bl(xnb, cols0 + jr * 128, (4 - jr) * 128), [pbk[bk]],
                              out=ps[:, jr * 128:512], lhsT=KTh[s][pr, kb * 128:(kb + 1) * 128],
                              rhs=xn[pr, h, cols0 + jr * 128:cols0 + 512], start=True, stop=True)
                    return alt, j0

                def emit_exp_pv(kb, alt, j0):
                    for c in range(2):
                        bk = 2 * c + alt
                        A("activation", [pbk[bk]], [ptb[c][alt]], out=PT[c][alt][:, j0 * 128:512],
                          in_=pall[:, bk, j0 * 128:512], func=AF.Exp, scale=SCALE)
                    for c in range(2):
                        for j in range(j0, 4):
                            obank = 4 + (j * 2 + c) // 3
                            st = (kb == 0) and (obank not in opened)
                            opened.add(obank)
                            T("matmul", [ptb[c][alt], vhb[s]], [ob[j][c]], out=oreg(j, c),
                              lhsT=PT[c][alt][:, j * 128:(j + 1) * 128], rhs=Vh[s][:, kb, :],
                              start=st, stop=(kb == i0 + j))

                pend = emit_qk(0)
                for kb in range(nkbs):
                    nxt = emit_qk(kb + 1) if kb + 1 < nkbs else None
                    emit_exp_pv(kb, pend[0], pend[1])
                    pend = nxt
                A("activation", [pbk[4]], [sqb], out=ocp[:, 0:387], in_=pall[:, 4, 0:387], func=AF.Copy)
                V("tensor_copy", [pbk[5]], [sqb], out=ocp[:, 512:512 + 387], in_=pall[:, 5, 0:387])
                A("activation", [pbk[6]], [sqb], out=ocp[:, 1024:1024 + 258], in_=pall[:, 6, 0:258], func=AF.Copy)
                for j in range(4):
                    o0, o1 = oreg_sb(j, 0), oreg_sb(j, 1)
                    col = cols0 + j * 128
                    V("reciprocal", [sqb], [epb], out=sc[:, 0:1], in_=o0[:, 128:129])
                    V("reciprocal", [sqb], [epb], out=sc[:, 1:2], in_=o1[:, 128:129])
                    V("tensor_tensor", [epb, cb], [epb], out=sc[:, 1:2], in0=sc[:, 1:2], in1=neglam[:], op=ALU.mult)
                    V("tensor_scalar", [sqb, epb], [epb], out=t1, in0=o1[:, 0:128], scalar1=sc[:, 1:2],
                      scalar2=None, op0=ALU.mult)
                    V("scalar_tensor_tensor", [sqb, epb], [epb], out=dd, in0=o0[:, 0:128], scalar=sc[:, 0:1],
                      in1=t1, op0=ALU.mult, op1=ALU.add)
                    V("memset", [], [epb], ap=sc[:, 2:3], constant=0.0)
                    A("activation", [epb], [epb], out=junk, in_=dd, func=AF.Square, accum_out=sc[:, 2:3])
                    A("activation", [epb], [epb], out=sc[:, 3:4], in_=sc[:, 2:3], func=AF.Sqrt, scale=1.0 / 128, bias=EPS)
                    V("reciprocal", [epb], [epb], out=sc[:, 4:5], in_=sc[:, 3:4])
                    V("scalar_tensor_tensor", [epb, cb], [epb], out=dn, in0=dd, scalar=sc[:, 4:5], in1=subln_b[:],
                      op0=ALU.mult, op1=ALU.mult)
                    pbv = pall[:, 7, :].bitcast(BF16)
                    T("transpose", [epb, cb], [pbk[7]], out=pbv[:, 0:128], in_=dn, identity=identb[:])
                    A("activation", [pbk[7]], bl(mob, col, 128), out=mixo[:, h, col:col + 128], in_=pbv[:, 0:128],
                      func=AF.Copy)

    def sample_attention():
        mk = P.new_phase()
        cv_ = Carve()
        NK = int(os.environ.get("MK_NK", "3"))
        kpg = [kvpg[i][:] for i in range(NK)]
        vpg = [kvpg[NK + i][:] for i in range(NK)]
        ktpg = [cv_.bf16(D) for _ in range(2)]
        S_sb = cv_.f32(2048 + 64)
        Pb = cv_.bf16(2048 + 64)
        PTs = cv_.bf16(D)
        PTn = cv_.bf16(64)
        qpad = cv_.bf16(576)
        tsel = S_sb[:, 0:D]
        osel = cv_.f32(128)
        mx = cv_.f32(8)
        dsall = cv_.f32(8 * 64)
        smask = cv_.f32(D).rearrange("p (s k) -> p s k", s=16)
        kpgb = [mk() for _ in range(NK)]
        vpgb = [mk() for _ in range(NK)]
        ktpgb = [mk(), mk()]
        ssb, pbb, ptsb, ptnb, qpb, oselb, mxb, dsb, smkb = [mk() for _ in range(9)]
        tselb = ssb
        DS("smask", [], [smkb], out=smask[0:64], in_=c_smask)
        V("memset", [], [qpb], ap=qpad, constant=0.0)
        qpv = qpad.rearrange("p (j c) -> p j c", c=72)
        sa_n = int(os.environ.get("MK_SA_N", str(NSS)))
        sa_nov = bool(os.environ.get("MK_SA_NOV"))
        for s in range(sa_n):
            w = s // 16
            sl = s % 16
            qc = 4 * s
            V("tensor_copy", bl(xnb, qc, 4), [qpb], out=qpv[0:64, :, 0:4], in_=xn[0:64, :, qc:qc + 4])
            V("tensor_copy", bl(xnb, qc, 4), [qpb], out=qpv[64:128, :, 4:8], in_=xn[64:128, :, qc:qc + 4])
            for pg in range(16):
                ks = rr("kpg", NK)
                P.op("gpsimd", (lambda ks=ks, col=s * 16 + pg: nc.gpsimd.indirect_dma_start(
                    out=kpg[ks], out_offset=None, in_=ck,
                    in_offset=bass.IndirectOffsetOnAxis(ap=idxall[:, col:col + 1], axis=0))),
                    [cb], [kpgb[ks]], dma=f"kpg{ks}")
                bk = 6 + rr("tp")
                pbv = pall[:, bk, :].bitcast(BF16)
                for j in range(8):
                    T("transpose", [kpgb[ks], cb], [pbk[bk]], out=pbv[:, j * 128:(j + 1) * 128],
                      in_=kpg[ks][:, j * 128:(j + 1) * 128], identity=identb[:])
                ka = rr("ktpg")
                if pg % 2 == 0:
                    A("activation", [pbk[bk]], [ktpgb[ka]], out=ktpg[ka], in_=pbv, func=AF.Copy)
                else:
                    V("tensor_copy", [pbk[bk]], [ktpgb[ka]], out=ktpg[ka], in_=pbv)
                quad = pg // 4
                sbk = 0 + (quad % 2)
                for j in range(8):
                    T("matmul", [qpb, ktpgb[ka]], [pbk[sbk]], out=pall[0:64, sbk, (pg % 4) * 128:(pg % 4 + 1) * 128],
                      lhsT=qpad[:, j * 64:(j + 1) * 64], rhs=ktpg[ka][:, j * 128:(j + 1) * 128],
                      start=(j == 0), stop=(j == 7))
                if pg % 4 == 3:
                    V("tensor_reduce", [pbk[sbk]], [mxb], out=mx[0:64, quad:quad + 1], in_=pall[0:64, sbk, :],
                      axis=AX.X, op=ALU.max)
                    A("activation", [pbk[sbk]], [ssb], out=S_sb[0:64, quad * 512:(quad + 1) * 512],
                      in_=pall[0:64, sbk, :], func=AF.Copy)
            if sa_nov:
                continue
            for pg in range(16):
                vs_ = rr("vpg", NK)
                P.op("gpsimd", (lambda vs_=vs_, col=s * 16 + pg: nc.gpsimd.indirect_dma_start(
                    out=vpg[vs_], out_offset=None, in_=cv,
                    in_offset=bass.IndirectOffsetOnAxis(ap=idxall[:, col:col + 1], axis=0))),
                    [cb], [vpgb[vs_]], dma=f"vpg{vs_}")
                if pg == 0:
                    for j in range(8):
                        T("matmul", [qpb, ksb], [pbk[2]], out=pall[0:64, 2, 0:64], lhsT=qpad[:, j * 64:(j + 1) * 64],
                          rhs=KsT[:, j, 64 * w:64 * w + 64], start=(j == 0), stop=(j == 7))
                    V("tensor_tensor", [pbk[2], smkb], [ssb], out=S_sb[0:64, 2048:2112], in0=pall[0:64, 2, 0:64],
                      in1=smask[0:64, sl, :], op=ALU.add)
                    V("tensor_reduce", [ssb], [mxb], out=mx[0:64, 4:5], in_=S_sb[0:64, 2048:2112], axis=AX.X, op=ALU.max)
                    V("tensor_reduce", [mxb], [mxb], out=mx[0:64, 5:6], in_=mx[0:64, 0:5], axis=AX.X, op=ALU.max)
                    V("tensor_scalar", [mxb], [mxb], out=mx[0:64, 6:7], in0=mx[0:64, 5:6], scalar1=-SCALE, scalar2=None,
                      op0=ALU.mult)
                    V("memset", [], [mxb], ap=mx[0:64, 7:8], constant=0.0)
                    A("activation", [ssb, mxb], [pbb, mxb], out=Pb[0:64, :], in_=S_sb[0:64, :], func=AF.Exp, scale=SCALE,
                      bias=mx[0:64, 6:7], accum_out=mx[0:64, 7:8])
                    pbv = pall[:, 3, :].bitcast(BF16)
                    for b in range(16):
                        T("transpose", [pbb, cb], [pbk[3]], out=pbv[:, b * 64:(b + 1) * 64], in_=Pb[0:64, b * 128:(b + 1) * 128],
                          identity=identb[0:64, 0:64])
                    pbn = pall[:, 2, :].bitcast(BF16)
                    T("transpose", [pbb, cb], [pbk[2]], out=pbn[0:64, 512:576], in_=Pb[0:64, 2048:2112],
                      identity=identb[0:64, 0:64])
                    A("activation", [pbk[3]], [ptsb], out=PTs, in_=pbv, func=AF.Copy)
                    V("tensor_copy", [pbk[2]], [ptnb], out=PTn[0:64, :], in_=pbn[0:64, 512:576])
                for half in range(2):
                    T("matmul", [ptsb, vpgb[vs_]], [pbk[4 + half]], out=pall[0:64, 4 + half, :],
                      lhsT=PTs[:, pg * 64:(pg + 1) * 64], rhs=vpg[vs_][:, half * 512:(half + 1) * 512],
                      start=(pg == 0), stop=False)
            for half in range(2):
                T("matmul", [ptnb, vsb], [pbk[4 + half]], out=pall[0:64, 4 + half, :], lhsT=PTn[0:64, :],
                  rhs=Vs_bf[0:64, w, half * 512:(half + 1) * 512], start=False, stop=True)
            osrc = pall[0:64, 4:6, :].rearrange("p b (h v) -> p (b h) v", v=128)
            V("tensor_tensor", [pbk[4], pbk[5], cb], [tselb], out=tsel[0:64].rearrange("p (h v) -> p h v", h=8), in0=osrc,
              in1=hmask[:].unsqueeze(2).to_broadcast([64, 8, 128]), op=ALU.mult)
            V("tensor_reduce", [tselb], [oselb], out=osel[0:64], in_=tsel[0:64].rearrange("p (h v) -> p v h", h=8),
              axis=AX.X, op=ALU.add)
            V("reciprocal", [mxb], [mxb], out=mx[0:64, 7:8], in_=mx[0:64, 7:8])
            V("tensor_scalar", [oselb, mxb], [oselb], out=osel[0:64], in0=osel[0:64], scalar1=mx[0:64, 7:8], scalar2=None,
              op0=ALU.mult)
            T("transpose", [oselb, cb], [pbk[7]], out=pall[:, 7, 0:64], in_=osel[0:64], identity=identf[0:64, 0:64])
            A("activation", [pbk[7]], [oselb], out=osel[:, 0:64], in_=pall[:, 7, 0:64], func=AF.Copy)
            ot = osel[:, 0:64].rearrange("p (h c q) -> p h c q", h=8, c=2)
            V("scalar_tensor_tensor", [oselb, cb], [dsb], out=dsall.rearrange("p (h t) -> p h t", h=8)[:, :, 4 * sl:4 * sl + 4],
              in0=ot[:, :, 1, :], scalar=neglam[:, 0:1], in1=ot[:, :, 0, :], op0=ALU.mult, op1=ALU.add)
            if sl == 15:
                c0 = 64 * w
                A("activation", [dsb], [sqb], out=sq[:, 0, :], in_=dsall, func=AF.Square)
                T("matmul", [sqb, cb], [pbk[6]], out=pall[:, 6, :], lhsT=onesb[:], rhs=sq[:, 0, :], start=True, stop=True)
                A("activation", [pbk[6]], [tfb[0]], out=tf32[0][:], in_=pall[:, 6, :], func=AF.Sqrt, scale=1.0 / 128, bias=EPS)
                V("reciprocal", [tfb[0]], [tfb[0]], out=tf32[0][:], in_=tf32[0][:])
                V("scalar_tensor_tensor", [dsb, tfb[0], cb], bl(mob, c0, 64), out=mixo[:, :, c0:c0 + 64],
                  in0=dsall.rearrange("p (h t) -> p h t", h=8), scalar=subln_p[:, 0:1],
                  in1=tf32[0][:].rearrange("p (h t) -> p h t", h=8), op0=ALU.mult, op1=ALU.mult)

    def final_phase(grp, yrow):
        mk = P.new_phase()
        cv_ = Carve()
        gfin = cv_.f32(D)
        junk = cv_.bf16(D)
        sc = cv_.f32(8)
        gb_, jb, scb = mk(), mk(), mk()
        DS("gfin", [], [gb_], out=gfin, in_=w_final_norm.rearrange("(o d) -> o d", o=1).partition_broadcast(128))
        for (off, nt) in grp.subs:
            for half in range(2):
                for c in range(4):
                    n = half * 4 + c
                    T("transpose", bl(xb, off, nt) + [cb], [pbk[half]], out=pall[0:nt, half, c * 128:(c + 1) * 128],
                      in_=x[:, n, off:off + nt], identity=identf[:])
            src = pall[0:nt, 0:2, :].rearrange("p b f -> p (b f)")
            V("memset", [], [scb], ap=sc[0:nt, 0:1], constant=0.0)
            A("activation", [pbk[0], pbk[1]], [jb, scb], out=junk[0:nt], in_=src, func=AF.Square, accum_out=sc[0:nt, 0:1])
            A("activation", [scb], [scb], out=sc[0:nt, 1:2], in_=sc[0:nt, 0:1], func=AF.Sqrt, scale=1.0 / D, bias=EPS)
            V("reciprocal", [scb], [scb], out=sc[0:nt, 2:3], in_=sc[0:nt, 1:2])
            s = rr("stg")
            V("scalar_tensor_tensor", [pbk[0], pbk[1], scb, gb_], [stgb[s]], out=stg[s][0:nt, :], in0=src,
              scalar=sc[0:nt, 2:3], in1=gfin[0:nt], op0=ALU.mult, op1=ALU.mult)
            DS(f"stg{s}", [stgb[s]], [], out=yrow(off, nt), in_=stg[s][0:nt, :])

    pgrp = Grp("p", GP)
    sgrp = Grp("s", NSMP)

    def layer0(grp, si, g, xrow, prow):
        for (off, nt) in grp.subs:
            load_tm_to_fm(xrow(off, nt), nt, D, x, off, bl(xb, off, nt))
        stage("l0.load")
        ffn(w_ffn1_gu[0], w_ffn1_dn[0], grp.tiles, V_FFN1N0)
        stage("l0.ffn1")
        rmsnorm(grp.tiles, V_MIXN0)
        stage("l0.norm")
        mixer0(grp, si, g)
        stage("l0.mixer")
        ffn(w_ffn2_gu[0], w_ffn2_dn[0], grp.tiles, V_FFN2N0)
        stage("l0.ffn2")
        ple(0, grp, prow, V_PLEN0)
        stage("l0.ple")
        kv_phase(grp, si, g)
        stage("l0.kv")

    def layer1(grp, g, prow, yrow):
        ffn(w_ffn1_gu[1], w_ffn1_dn[1], grp.tiles, V_FFN1N1)
        stage("l1.ffn1")
        rmsnorm(grp.tiles, V_MIXN1)
        q_phase(grp, g)
        stage("l1.q")
        if grp.kind == "p":
            attention(g)
        else:
            sample_attention()
        dbg = os.environ.get("MK_DBG", "")
        if dbg in ("attn", "q") and grp.kind == "p":
            srcT = mixo if dbg == "attn" else xn
            for k in range(8):
                V("tensor_copy", bl(mob, 0, GP) + bl(xnb, 0, GP), bl(xb, 0, GP), out=x[:, k, 0:GP], in_=srcT[:, k, 0:GP])
            for (off, nt) in grp.subs:
                fm_to_dram(x[:, :, off:off + nt], nt, yrow(off, nt), bl(xb, off, nt))
            raise _Stop()
        stage("l1.attn")
        s = next_wslot()
        wov = load_w_full(w_o[0], s, 8)
        dense_add(s, wov, mixo, mob, grp.tiles, 8)
        stage("l1.wo")
        ffn(w_ffn2_gu[1], w_ffn2_dn[1], grp.tiles, V_FFN2N1)
        stage("l1.ffn2")
        ple(1, grp, prow, V_PLEN1)
        stage("l1.ple")
        final_phase(grp, yrow)
        stage("l1.final")

    def program():
        setup()
        stage("setup")
        for si in range(0 if not os.environ.get("MK_SKIPP") else NPS, NPS):
            for g in range(4):
                base = si * SEQ + g * GP
                layer0(pgrp, si, g,
                       (lambda off, nt, base=base: xp[base + off:base + off + nt, :]),
                       (lambda off, nt, base=base: pp0[base + off:base + off + nt, :]))
            for g in range(4):
                base = si * SEQ + g * GP
                DS("x1ld", [x1b[g]], bl(xb, 0, GP), out=x[:, :, 0:GP], in_=x1scr[:, :, g * GP:(g + 1) * GP])
                layer1(pgrp, g,
                       (lambda off, nt, base=base: pp1[base + off:base + off + nt, :]),
                       (lambda off, nt, base=base: y_p[base + off:base + off + nt, :]))
        layer0(sgrp, 0, 0, (lambda off, nt: xs[off:off + nt, :]), (lambda off, nt: ps0[off:off + nt, :]))
        layer1(sgrp, 0, (lambda off, nt: ps1[off:off + nt, :]), (lambda off, nt: y_s[off:off + nt, :]))

    try:
        program()
    except _Stop:
        pass

    P.emit_all()
    P.close()
    return nc


_CACHE = {}


def _rope_tables(pos):
    half = 32
    inv = np.power(np.float32(10000.0), -np.arange(half, dtype=np.float32) * np.float32(2.0) / np.float32(64.0)).astype(np.float32)
    ang = pos.astype(np.float32)[:, None] * inv[None, :]
    return np.cos(ang).astype(np.float32), np.sin(ang).astype(np.float32)


def kernel(**inp):
    f32 = np.float32
    npool = inp["cache_k"].shape[0]
    key = ("nc", npool)
    if key not in _CACHE:
        _CACHE[key] = build_program(npool)
    nc = _CACHE[key]

    ck = np.ascontiguousarray(inp["cache_k"], dtype=f32).reshape(npool * 128, D)
    cv = np.ascontiguousarray(inp["cache_v"], dtype=f32).reshape(npool * 128, D)
    weights = ["ffn1_norm", "ffn1_w_gu", "ffn1_w_down", "mix_norm", "rg_w_in", "rg_conv_w", "rg_conv_b", "rg_w_a",
               "rg_b_a", "rg_w_i", "rg_b_i", "rg_lambda", "rg_w_out", "kv_norm", "w_kv", "attn_w_q", "lambda_q1",
               "lambda_k1", "lambda_q2", "lambda_k2", "attn_subln", "attn_w_o", "ffn2_norm", "ffn2_w_gu",
               "ffn2_w_down", "ple_norm", "ple_w_gate", "ple_w_proj", "final_norm"]
    wd = {k: np.ascontiguousarray(inp[k], dtype=f32) for k in weights}

    cosk, sink = _rope_tables(np.arange(SEQ))
    ropek_c = np.ascontiguousarray(cosk.reshape(32, 128, 32).transpose(1, 0, 2))
    ropek_s = np.ascontiguousarray(sink.reshape(32, 128, 32).transpose(1, 0, 2))
    coss, sins = _rope_tables(2048 + (np.arange(128) % 4))
    tri = np.where(np.arange(128)[:, None] <= np.arange(128)[None, :], 0.0, NEG).astype(f32)
    smask = np.full((64, 16, 64), NEG, f32)
    for s in range(16):
        for r in range(64):
            q = r % 4
            for k in range(q + 1):
                smask[r, s, 4 * s + k] = 0.0
    hmask = np.zeros((64, 8), f32)
    for r in range(64):
        hmask[r, r // 8] = 1.0

    xp_all = np.ascontiguousarray(inp["x_prompt"], dtype=f32)
    xs_all = np.ascontiguousarray(inp["x_sample"], dtype=f32)
    pp_all = np.ascontiguousarray(inp["p_prompt"], dtype=f32)
    ps_all = np.ascontiguousarray(inp["p_sample"], dtype=f32)
    pt_all = np.ascontiguousarray(inp["page_table"], dtype=np.int32)
    sc_all = np.ascontiguousarray(inp["state_conv"], dtype=f32)
    sh_all = np.ascontiguousarray(inp["state_rglru"], dtype=f32)

    in_maps = []
    for c in range(NCORES):
        ps_ = slice(c * NPS, (c + 1) * NPS)
        ss_ = slice(c * NSS, (c + 1) * NSS)
        m = dict(wd)
        m.update({
            "xp": xp_all[ps_].reshape(NPS * SEQ, D), "xs": xs_all[ss_].reshape(NSMP, D),
            "pp0": pp_all[0, ps_].reshape(NPS * SEQ, 256), "pp1": pp_all[1, ps_].reshape(NPS * SEQ, 256),
            "ps0": ps_all[0, ss_].reshape(NSMP, 256), "ps1": ps_all[1, ss_].reshape(NSMP, 256),
            "ck": ck, "cv": cv, "pt": pt_all[ss_].reshape(1, NSS * 16),
            "sconv": sc_all[0, ss_].reshape(NSS * 3, D), "sh": sh_all[0, ss_],
            "ropek_c": ropek_c, "ropek_s": ropek_s, "ropes_c": coss, "ropes_s": sins,
            "tri": tri, "smask": smask, "hmask": hmask,
        })
        in_maps.append(m)

    res = run_bass_kernel_spmd(nc, in_maps, core_ids=list(range(NCORES)))
    R = res.results

    def cat(name):
        return np.concatenate([R[c][name] for c in range(NCORES)], axis=0)

    y_prompt = cat("y_p").reshape(4, SEQ, D)
    y_sample = cat("y_s").reshape(128, 4, D)
    nk_p = cat("nk_p").reshape(4, SEQ, 16, 64)
    nv_p = cat("nv_p").reshape(4, SEQ, 8, 128)
    nk_s = cat("nk_s").reshape(128, 4, 16, 64)
    nv_s = cat("nv_s").reshape(128, 4, 8, 128)
    nconv_p = cat("nconv_p").reshape(1, 4, 3, D)
    nh_p = cat("nh_p").reshape(1, 4, D)
    nconv_s = cat("nconv_s").reshape(1, 128, 3, D)
    nh_s = cat("nh_s").reshape(1, 128, D)
    return (y_prompt, y_sample, nk_p, nv_p, nk_s, nv_s, nconv_p, nh_p, nconv_s, nh_s)
```

```python
import contextlib
import math
import os
import numpy as np
import concourse.bass as bass
import concourse.mybir as mybir
from concourse.bass_utils import run_bass_kernel_spmd

F32 = mybir.dt.float32
BF16 = mybir.dt.bfloat16
I32 = mybir.dt.int32
AF = mybir.ActivationFunctionType
ALU = mybir.AluOpType
AX = mybir.AxisListType

ENGS = ("tensor", "vector", "scalar", "gpsimd", "sync")


class Buf:
    __slots__ = ("name", "w", "rs", "excl", "xacc")

    def __init__(self, name, rs=None, excl=False):
        self.name = name
        self.w = None
        self.rs = list(rs) if rs else []
        self.excl = excl
        self.xacc = {}


class Op:
    __slots__ = ("eng", "idx", "emit", "deps", "needs_inc", "count", "dma")

    def __init__(self, eng, idx, emit):
        self.eng = eng
        self.idx = idx
        self.emit = emit
        self.deps = []
        self.needs_inc = False
        self.count = 0
        self.dma = None


class Prog:
    def __init__(self, nc):
        self.nc = nc
        self.q = {e: [] for e in ENGS}
        self.dma_counts = {}
        self.stack = contextlib.ExitStack()
        self.nbuf = 0
        self.phase_bufs = []

    def buf(self, name=None):
        self.nbuf += 1
        return Buf(name or f"b{self.nbuf}")

    def new_phase(self):
        evs = set()
        for b in self.phase_bufs:
            if b.w is not None:
                evs.add(b.w)
            evs.update(b.rs)
        evs = list(evs)
        self.phase_bufs = []

        def mk(name=None):
            self.nbuf += 1
            b = Buf(name or f"pb{self.nbuf}", rs=evs)
            self.phase_bufs.append(b)
            return b
        return mk

    def sbuf(self, name, shape, dtype):
        return self.stack.enter_context(self.nc.sbuf_tensor("sb_" + name, list(shape), dtype))

    def psum(self, name, shape, dtype=F32):
        return self.stack.enter_context(self.nc.psum_tensor("ps_" + name, list(shape), dtype))

    def op(self, eng, emit, reads=(), writes=(), dma=None):
        q = self.q[eng]
        o = Op(eng, len(q), emit)
        deps = set()
        for b in reads:
            if b.w is not None:
                deps.add(b.w)
        for b in writes:
            w = b.w
            if w is not None:
                if dma is None and w[0] == "E" and w[1] == eng:
                    pass
                elif dma is not None and w[0] == "D" and w[1] == dma:
                    pass
                else:
                    deps.add(w)
            for r in b.rs:
                if dma is None and r[0] == "E" and r[1] == eng:
                    continue
                deps.add(r)
        for b in list(reads) + list(writes):
            if b.excl:
                for e2, ev2 in b.xacc.items():
                    if e2 != eng:
                        deps.add(ev2)
        if dma is not None:
            c = self.dma_counts.get(dma, 0) + 16
            self.dma_counts[dma] = c
            o.dma = (dma, c)
            ev = ("D", dma, c)
        else:
            ev = ("E", eng, o.idx)
        best = {}
        for d in deps:
            k = (d[0], d[1])
            if k not in best or d[2] > best[k][2]:
                best[k] = d
        deps = list(best.values())
        for d in deps:
            if d[0] == "E":
                self.q[d[1]][d[2]].needs_inc = True
        o.deps = sorted(deps, key=lambda d: (d[0], str(d[1]), d[2]))
        for b in reads:
            b.rs = [r for r in b.rs if not (r[0] == ev[0] and r[1] == ev[1])]
            b.rs.append(ev)
        for b in writes:
            b.w = ev
            b.rs = []
        for b in list(reads) + list(writes):
            if b.excl:
                b.xacc[eng] = ev
        q.append(o)
        return o

    def emit_all(self, final_wait_engine="sync"):
        nc = self.nc
        SEG = 30000
        nseg = {}
        for e in ENGS:
            c = 0
            for o in self.q[e]:
                if o.needs_inc and o.dma is None:
                    c += 1
                    o.count = c
            nseg[e] = (c + SEG - 1) // SEG + 1
        esem = {e: [self.stack.enter_context(nc.semaphore(f"es_{e}{i}")) for i in range(nseg[e])] for e in ENGS}
        dsem = {n: self.stack.enter_context(nc.semaphore(f"ds_{n}")) for n in self.dma_counts}
        engobj = {"tensor": nc.tensor, "vector": nc.vector, "scalar": nc.scalar,
                  "gpsimd": nc.gpsimd, "sync": nc.sync}
        prog = self

        def run_engine(e):
            eng = engobj[e]
            waited = {}
            for o in prog.q[e]:
                for d in o.deps:
                    if d[0] == "E":
                        cnt_ = prog.q[d[1]][d[2]].count
                        seg_ = (cnt_ - 1) // SEG
                        sem = esem[d[1]][seg_]
                        val = cnt_ - seg_ * SEG
                        key = ("E", d[1], seg_)
                    else:
                        sem = dsem[d[1]]
                        val = d[2]
                        key = ("D", d[1])
                    if waited.get(key, 0) >= val:
                        continue
                    waited[key] = val
                    eng.wait_ge(sem, val)
                ins = o.emit()
                if o.dma is not None:
                    ins.then_inc(dsem[o.dma[0]], 16)
                elif o.needs_inc:
                    ins.then_inc(esem[e][(o.count - 1) // SEG], 1)
            if e == final_wait_engine:
                for name, c in prog.dma_counts.items():
                    if waited.get(("D", name), 0) < c:
                        eng.wait_ge(dsem[name], c)
                for e2 in ENGS:
                    if e2 == e:
                        continue
                    last = 0
                    for o in prog.q[e2]:
                        if o.count > last:
                            last = o.count
                    if last > 0:
                        seg_ = (last - 1) // SEG
                        if last - seg_ * SEG > waited.get(("E", e2, seg_), 0):
                            eng.wait_ge(esem[e2][seg_], last - seg_ * SEG)

        with nc.Block() as block:
            @block.tensor
            def _(t):
                run_engine("tensor")

            @block.vector
            def _(v):
                run_engine("vector")

            @block.scalar
            def _(s):
                run_engine("scalar")

            @block.gpsimd
            def _(g):
                run_engine("gpsimd")

            @block.sync
            def _(s):
                run_engine("sync")

    def close(self):
        self.stack.close()


NCORES = int(os.environ.get("MK_CORES", "4"))
NPS = 4 // NCORES
NSS = 128 // NCORES
NSMP = NSS * 4
D = 1024
NCH = 8
DFF = 2816
NFF = 22
SEQ = 4096
GP = 1024
TMAX = GP
EPS = 1e-6
NEG = -30000.0
LAMBDA_INIT = 0.8 - 0.6 * math.exp(-0.3 * 1)
SCALE = 0.125
ARENA_W = 9728
FPARTS = [(0, 3), (3, 3), (6, 3), (9, 3), (12, 3), (15, 3), (18, 2), (20, 2)]
WSLOT = 9216

V_FFN1N0, V_MIXN0, V_FFN2N0, V_PLEN0, V_KVN = 0, 1, 2, 3, 4
V_CW0, V_CB, V_BA, V_BI, V_LAM = 5, 9, 10, 11, 12
V_FFN1N1, V_MIXN1, V_FFN2N1, V_PLEN1 = 13, 14, 15, 16


class Grp:
    def __init__(self, kind, T):
        self.kind = kind
        self.T = T
        self.tiles = [(o, min(512, T - o)) for o in range(0, T, 512)]
        ss = 64 if kind == "s" else 128
        self.subs = [(o, min(ss, T - o)) for o in range(0, T, ss)]


class _Stop(Exception):
    pass


def build_program(npool):
    nc = bass.Bass("TRN2", target_bir_lowering=False, dynamic_dma_scratch_size=24576)
    P = Prog(nc)
    stop_at = int(os.environ.get("MK_STOP", "1000000"))
    stage_ctr = [0]

    def stage(name):
        stage_ctr[0] += 1
        if stage_ctr[0] == stop_at:
            print("MK_STOP at stage", stage_ctr[0], name, flush=True)
            raise _Stop()

    def din(name, shape, dt=F32):
        return nc.dram_tensor(name, list(shape), dt, kind="ExternalInput").ap()

    def dout(name, shape, dt=F32):
        return nc.dram_tensor(name, list(shape), dt, kind="ExternalOutput").ap()

    def dscr(name, shape, dt):
        return nc.dram_tensor(name, list(shape), dt, kind="Internal").ap()

    xp = din("xp", [NPS * SEQ, D])
    xs = din("xs", [NSMP, D])
    pp0 = din("pp0", [NPS * SEQ, 256])
    pp1 = din("pp1", [NPS * SEQ, 256])
    ps0 = din("ps0", [NSMP, 256])
    ps1 = din("ps1", [NSMP, 256])
    ck = din("ck", [npool * 128, D])
    cv = din("cv", [npool * 128, D])
    pt = din("pt", [1, NSS * 16], I32)
    sconv = din("sconv", [NSS * 3, D])
    sh = din("sh", [NSS, D])
    w_ffn1_norm = din("ffn1_norm", [2, D])
    w_ffn1_gu = din("ffn1_w_gu", [2, D, 2 * DFF])
    w_ffn1_dn = din("ffn1_w_down", [2, DFF, D])
    w_mix_norm = din("mix_norm", [2, D])
    w_rg_in = din("rg_w_in", [1, D, 2 * D])
    w_rg_conv_w = din("rg_conv_w", [1, 4, D])
    w_rg_conv_b = din("rg_conv_b", [1, D])
    w_rg_a = din("rg_w_a", [1, 8, 128, 128])
    w_rg_ba = din("rg_b_a", [1, D])
    w_rg_i = din("rg_w_i", [1, 8, 128, 128])
    w_rg_bi = din("rg_b_i", [1, D])
    w_rg_lam = din("rg_lambda", [1, D])
    w_rg_out = din("rg_w_out", [1, D, D])
    w_kv_norm = din("kv_norm", [D])
    w_kv = din("w_kv", [D, 2 * D])
    w_q = din("attn_w_q", [1, D, D])
    w_lq1 = din("lambda_q1", [1, 64])
    w_lk1 = din("lambda_k1", [1, 64])
    w_lq2 = din("lambda_q2", [1, 64])
    w_lk2 = din("lambda_k2", [1, 64])
    w_subln = din("attn_subln", [1, 128])
    w_o = din("attn_w_o", [1, D, D])
    w_ffn2_norm = din("ffn2_norm", [2, D])
    w_ffn2_gu = din("ffn2_w_gu", [2, D, 2 * DFF])
    w_ffn2_dn = din("ffn2_w_down", [2, DFF, D])
    w_ple_norm = din("ple_norm", [2, D])
    w_ple_gate = din("ple_w_gate", [2, D, D])
    w_ple_proj = din("ple_w_proj", [2, 256, D])
    w_final_norm = din("final_norm", [D])
    c_ropek_c = din("ropek_c", [128, 32, 32])
    c_ropek_s = din("ropek_s", [128, 32, 32])
    c_ropes_c = din("ropes_c", [128, 32])
    c_ropes_s = din("ropes_s", [128, 32])
    c_tri = din("tri", [128, 128])
    c_smask = din("smask", [64, 16, 64])
    c_hmask = din("hmask", [64, 8])

    y_p = dout("y_p", [NPS * SEQ, D])
    y_s = dout("y_s", [NSMP, D])
    nk_p = dout("nk_p", [NPS * SEQ, D])
    nv_p = dout("nv_p", [NPS * SEQ, D])
    nk_s = dout("nk_s", [NSMP, D])
    nv_s = dout("nv_s", [NSMP, D])
    nconv_p = dout("nconv_p", [NPS * 3, D])
    nh_p = dout("nh_p", [NPS, D])
    nconv_s = dout("nconv_s", [NSS * 3, D])
    nh_s = dout("nh_s", [NSS, D])

    ktscr = dscr("ktscr", [128, 8, SEQ], BF16)
    vscr = dscr("vscr", [128, 32, 8 * 129], BF16)
    x1scr = dscr("x1scr", [128, 8, SEQ], F32)

    x = P.sbuf("x", [128, NCH, TMAX], F32)
    xn = P.sbuf("xn", [128, NCH, TMAX], BF16)
    mixo = P.sbuf("mixo", [128, NCH, TMAX], BF16)
    wsl = [P.sbuf(f"wsl{i}", [128, WSLOT], BF16) for i in range(2)]
    wg = P.sbuf("wg", [128, 2, 8, 128], BF16)
    vecs = P.sbuf("vecs", [128, 136], F32)
    vrows = P.sbuf("vrows", [128, 128], F32)
    vrows2 = P.sbuf("vrows2", [8, 128], F32)
    identf = P.sbuf("identf", [128, 128], F32)
    identb = P.sbuf("identb", [128, 128], BF16)
    onesb = P.sbuf("onesb", [128, 128], BF16)
    sq = P.sbuf("sq", [128, NCH, 512], BF16)
    tf32 = [P.sbuf(f"tf32_{i}", [128, 512], F32) for i in range(2)]
    sg = [P.sbuf(f"sg{i}", [128, 512], BF16) for i in range(2)]
    hb = [P.sbuf(f"hb{i}", [128, 3, 512], BF16) for i in range(2)]
    pT = P.sbuf("pT", [128, 2, TMAX], BF16)
    stg = [P.sbuf(f"stg{i}", [128, D], F32) for i in range(2)]
    spneg8 = P.sbuf("spneg8", [128, 8], F32)
    hcarry = P.sbuf("hcarry", [128, 8], F32)
    ucarry = P.sbuf("ucarry", [128, 8, 3], F32)
    subln_b = P.sbuf("subln_b", [128, 128], F32)
    subln_p = P.sbuf("subln_p", [128, 1], F32)
    neglam = P.sbuf("neglam", [128, 1], F32)
    lamt = P.sbuf("lamt", [128, 4, 64], F32)
    lams = P.sbuf("lams", [128, 4], F32)
    ropes_c = P.sbuf("ropes_c", [128, 32], F32)
    ropes_s = P.sbuf("ropes_s", [128, 32], F32)
    tri = P.sbuf("tri", [128, 128], BF16)
    hmask = P.sbuf("hmask", [64, 8], F32)
    idxall = P.sbuf("idxall", [128, NSS * 16], I32)
    iota_p = P.sbuf("iota_p", [128, 1], F32)
    KsT = P.sbuf("KsT", [128, 8, NSMP], BF16)
    Vs_bf = P.sbuf("Vs_bf", [64, NSMP // 64, D], BF16)
    small = P.sbuf("small", [128, 16], F32)
    kvpg = [P.sbuf(f"kvpg{i}", [128, D], BF16) for i in range(6)]
    arena = P.sbuf("arena", [128, ARENA_W], F32)
    pall = P.psum("pall", [128, 8, 512], F32)

    def V(name, r, w, **kw):
        P.op("vector", lambda: getattr(nc.vector, name)(**kw), r, w)

    def A(name, r, w, **kw):
        P.op("scalar", lambda: getattr(nc.scalar, name)(**kw), r, w)

    def T(name, r, w, **kw):
        P.op("tensor", lambda: getattr(nc.tensor, name)(**kw), r, w)

    def G(name, r, w, **kw):
        P.op("gpsimd", lambda: getattr(nc.gpsimd, name)(**kw), r, w)

    def DS(sem, r, w, **kw):
        P.op("sync", lambda: nc.sync.dma_start(**kw), r, w, dma=sem)

    def DG(sem, r, w, **kw):
        P.op("gpsimd", lambda: nc.gpsimd.dma_start(**kw), r, w, dma=sem)

    cnt = {}

    def rr(name, n=2):
        v = cnt.get(name, 0)
        cnt[name] = v + 1
        return v % n

    class Carve:
        def __init__(self):
            self.off = 0

        def f32(self, words):
            a = arena[:, self.off:self.off + words]
            self.off += words
            assert self.off <= ARENA_W, self.off
            return a

        def bf16(self, elems):
            words = (elems + 1) // 2
            a = arena[:, self.off:self.off + words].bitcast(BF16)
            self.off += words
            assert self.off <= ARENA_W, self.off
            return a

    pbk = [P.buf(f"bank{i}") for i in range(8)]
    for b_ in pbk:
        b_.excl = True
    xb = [P.buf(f"x{j}") for j in range(8)]
    xnb = [P.buf(f"xn{j}") for j in range(8)]
    mob = [P.buf(f"mo{j}") for j in range(8)]
    wb = [P.buf(f"w{i}") for i in range(2)]
    wgb = P.buf("wg")
    cb = P.buf("const")
    sqb = P.buf("sq")
    tfb = [P.buf(f"tf{i}") for i in range(2)]
    sgb = [P.buf(f"sg{i}") for i in range(2)]
    hbb = [P.buf(f"hb{i}") for i in range(2)]
    pTb = P.buf("pT")
    stgb = [P.buf(f"stg{i}") for i in range(2)]
    hcb = P.buf("hcarry")
    ucb = P.buf("ucarry")
    ksb = P.buf("KsT")
    vsb = P.buf("Vs")
    ktb = [P.buf(f"kt{j}") for j in range(32)]
    x1b = [P.buf(f"x1_{j}") for j in range(4)]

    def sub_ids(off, n):
        return list(range(off // 128, (off + n + 127) // 128))

    def bl(lst, off, n):
        return [lst[j] for j in sub_ids(off, n)]

    def vcol(vi, n):
        return vecs[:, vi * 8 + n: vi * 8 + n + 1]

    def setup():
        def mk_ident():
            nc.gpsimd.memset(identf[:], 0.0)
            return nc.gpsimd.affine_select(out=identf[:], in_=identf[:], pattern=[[-1, 128]],
                                           compare_op=ALU.not_equal, fill=1.0, base=0, channel_multiplier=1)
        P.op("gpsimd", mk_ident, (), [cb])
        G("iota", [], [cb], out=iota_p[:], pattern=[[0, 1]], base=0, channel_multiplier=1,
          allow_small_or_imprecise_dtypes=True)
        V("tensor_copy", [cb], [cb], out=identb[:], in_=identf[:])
        V("memset", [], [cb], ap=onesb[:], constant=1.0)
        V("memset", [], [cb], ap=vrows[:], constant=0.0)
        vlist = [w_ffn1_norm[0], w_mix_norm[0], w_ffn2_norm[0], w_ple_norm[0], w_kv_norm,
                 w_rg_conv_w[0, 0], w_rg_conv_w[0, 1], w_rg_conv_w[0, 2], w_rg_conv_w[0, 3],
                 w_rg_conv_b[0], w_rg_ba[0], w_rg_bi[0], w_rg_lam[0],
                 w_ffn1_norm[1], w_mix_norm[1], w_ffn2_norm[1]]
        for i, v in enumerate(vlist):
            DS("const", [], [cb], out=vrows[i * 8:(i + 1) * 8, :], in_=v.rearrange("(n p) -> n p", p=128))
        DS("const", [], [cb], out=vrows2[:], in_=w_ple_norm[1].rearrange("(n p) -> n p", p=128))
        T("transpose", [cb], [pbk[0]], out=pall[:, 0, 0:128], in_=vrows[:], identity=identf[:])
        T("transpose", [cb], [pbk[0]], out=pall[:, 0, 128:136], in_=vrows2[:], identity=identf[0:8, 0:8])
        V("tensor_copy", [pbk[0]], [cb], out=vecs[:], in_=pall[:, 0, 0:136])
        lamv = vecs[:, V_LAM * 8:V_LAM * 8 + 8]
        A("activation", [cb], [cb], out=spneg8[:], in_=lamv, func=AF.Exp, scale=-1.0)
        A("activation", [cb], [cb], out=spneg8[:], in_=spneg8[:], func=AF.Ln, bias=1.0)
        V("tensor_scalar", [cb], [cb], out=spneg8[:], in0=spneg8[:], scalar1=-8.0, scalar2=None, op0=ALU.mult)
        DG("wg", [], [wgb], out=wg[:, 0], in_=w_rg_a[0].rearrange("n d e -> d n e"))
        DG("wg", [], [wgb], out=wg[:, 1], in_=w_rg_i[0].rearrange("n d e -> d n e"))
        for i, v in enumerate([w_lq1, w_lk1, w_lq2, w_lk2]):
            DS("const", [], [cb], out=lamt[:, i, :], in_=v.partition_broadcast(128))
        V("tensor_tensor", [cb], [cb], out=lamt[:, 0, :], in0=lamt[:, 0, :], in1=lamt[:, 1, :], op=ALU.mult)
        V("tensor_tensor", [cb], [cb], out=lamt[:, 2, :], in0=lamt[:, 2, :], in1=lamt[:, 3, :], op=ALU.mult)
        V("tensor_reduce", [cb], [cb], out=lams[:, 0:1], in_=lamt[:, 0, :], axis=AX.X, op=ALU.add)
        V("tensor_reduce", [cb], [cb], out=lams[:, 1:2], in_=lamt[:, 2, :], axis=AX.X, op=ALU.add)
        A("activation", [cb], [cb], out=lams[:, 0:2], in_=lams[:, 0:2], func=AF.Exp)
        V("scalar_tensor_tensor", [cb], [cb], out=neglam[:], in0=lams[:, 1:2], scalar=-LAMBDA_INIT, in1=lams[:, 0:1],
          op0=ALU.add, op1=ALU.subtract)
        DS("const", [], [cb], out=subln_b[:], in_=w_subln.partition_broadcast(128))
        V("tensor_scalar", [cb], [cb], out=subln_b[:], in0=subln_b[:], scalar1=1.0 - LAMBDA_INIT, scalar2=None, op0=ALU.mult)
        DS("const", [], [cb], out=small[:, 8:9], in_=w_subln.rearrange("o p -> p o"))
        V("tensor_scalar", [cb], [cb], out=subln_p[:], in0=small[:, 8:9], scalar1=1.0 - LAMBDA_INIT, scalar2=None, op0=ALU.mult)
        DS("const", [], [cb], out=ropes_c[:], in_=c_ropes_c)
        DS("const", [], [cb], out=ropes_s[:], in_=c_ropes_s)
        DG("constg", [], [cb], out=tri[:], in_=c_tri)
        DS("const", [], [cb], out=hmask[:], in_=c_hmask)
        idx_f = arena[:, 0:NSS * 16]
        idxb = P.buf("idxf")
        P.phase_bufs.append(idxb)
        DS("const", [], [cb], out=idxall[:], in_=pt.partition_broadcast(128))
        V("tensor_copy", [cb], [idxb], out=idx_f, in_=idxall[:])
        V("tensor_scalar", [cb, idxb], [idxb], out=idx_f, in0=idx_f, scalar1=128.0, scalar2=iota_p[:, 0:1],
          op0=ALU.mult, op1=ALU.add)
        V("tensor_copy", [idxb], [cb], out=idxall[:], in_=idx_f)

    def load_tm_to_fm(dram_rows, ntok, nfeat, dst3, col, dstbufs):
        s = rr("stg")
        nch = nfeat // 128
        DS(f"stg{s}", [], [stgb[s]], out=stg[s][0:ntok, 0:nfeat], in_=dram_rows)
        for b0 in range(0, nch, 4):
            nb = min(4, nch - b0)
            bk = 6 + rr("tp")
            for c in range(nb):
                T("transpose", [stgb[s], cb], [pbk[bk]], out=pall[:, bk, c * 128:c * 128 + ntok],
                  in_=stg[s][0:ntok, (b0 + c) * 128:(b0 + c + 1) * 128], identity=identf[0:ntok, 0:ntok])
            src = pall[:, bk, 0:nb * 128].rearrange("p (c t) -> p c t", c=nb)[:, :, 0:ntok]
            if rr("evq") == 0:
                A("activation", [pbk[bk]], dstbufs, out=dst3[:, b0:b0 + nb, col:col + ntok], in_=src, func=AF.Copy)
            else:
                V("tensor_copy", [pbk[bk]], dstbufs, out=dst3[:, b0:b0 + nb, col:col + ntok], in_=src)

    def rmsnorm(tiles, vi):
        for (off, n) in tiles:
            A("activation", bl(xb, off, n), [sqb], out=sq[:, :, 0:n], in_=x[:, :, off:off + n], func=AF.Square)
            bk = 6 + rr("nb")
            for k in range(8):
                T("matmul", [sqb, cb], [pbk[bk]], out=pall[:, bk, 0:n], lhsT=onesb[:], rhs=sq[:, k, 0:n],
                  start=(k == 0), stop=(k == 7))
            ri = rr("tf")
            rs = tf32[ri]
            A("activation", [pbk[bk]], [tfb[ri]], out=rs[:, 0:n], in_=pall[:, bk, 0:n], func=AF.Sqrt,
              scale=1.0 / D, bias=EPS)
            V("reciprocal", [tfb[ri]], [tfb[ri]], out=rs[:, 0:n], in_=rs[:, 0:n])
            for k in range(8):
                V("scalar_tensor_tensor", bl(xb, off, n) + [tfb[ri], cb], bl(xnb, off, n),
                  out=xn[:, k, off:off + n], in0=x[:, k, off:off + n], scalar=vcol(vi, k), in1=rs[:, 0:n],
                  op0=ALU.mult, op1=ALU.mult)

    def next_wslot():
        return rr("wslot")

    def ffn(wgu, wdn, tiles, vi):
        rmsnorm(tiles, vi)
        wguv = wgu.rearrange("(kc p) f -> p kc f", p=128)
        wdnv = wdn.rearrange("(fc p) d -> p fc d", p=128)
        pending = [None]

        def make_down(s, wdv, nf, hs, off, n):
            def go():
                for dm in range(8):
                    ab = 4 + rr("acc")
                    for fi in range(nf):
                        T("matmul", [wb[s], hbb[hs]], [pbk[ab]], out=pall[:, ab, 0:n],
                          lhsT=wdv[:, fi, dm * 128:(dm + 1) * 128], rhs=hb[hs][:, fi, 0:n],
                          start=(fi == 0), stop=(fi == nf - 1))
                    V("scalar_tensor_tensor", [pbk[ab]] + bl(xb, off, n), bl(xb, off, n),
                      out=x[:, dm, off:off + n], in0=pall[:, ab, 0:n], scalar=0.5, in1=x[:, dm, off:off + n],
                      op0=ALU.mult, op1=ALU.add)
            return go

        for (f0, nf) in FPARTS:
            s = next_wslot()
            W = wsl[s]
            ncol = nf * 128
            wgv = W[:, 0:8 * ncol].rearrange("p (k f) -> p k f", k=8)
            wuv = W[:, 8 * ncol:16 * ncol].rearrange("p (k f) -> p k f", k=8)
            wdv = W[:, 16 * ncol:16 * ncol + nf * D].rearrange("p (c d) -> p c d", c=nf)
            DG(f"w{s}", [], [wb[s]], out=wgv, in_=wguv[:, :, f0 * 128:f0 * 128 + ncol])
            DG(f"w{s}", [], [wb[s]], out=wuv, in_=wguv[:, :, DFF + f0 * 128:DFF + f0 * 128 + ncol])
            DG(f"w{s}", [], [wb[s]], out=wdv, in_=wdnv[:, f0:f0 + nf, :])
            for (off, n) in tiles:
                hs = rr("hb")
                for fi in range(nf):
                    gb = 0 + rr("gb")
                    ub = 2 + rr("ub")
                    for k in range(8):
                        T("matmul", [wb[s]] + bl(xnb, off, n), [pbk[gb]], out=pall[:, gb, 0:n],
                          lhsT=wgv[:, k, fi * 128:(fi + 1) * 128], rhs=xn[:, k, off:off + n],
                          start=(k == 0), stop=(k == 7))
                    for k in range(8):
                        T("matmul", [wb[s]] + bl(xnb, off, n), [pbk[ub]], out=pall[:, ub, 0:n],
                          lhsT=wuv[:, k, fi * 128:(fi + 1) * 128], rhs=xn[:, k, off:off + n],
                          start=(k == 0), stop=(k == 7))
                    si = rr("sg")
                    A("activation", [pbk[gb]], [sgb[si]], out=sg[si][:, 0:n], in_=pall[:, gb, 0:n], func=AF.Silu)
                    V("tensor_tensor", [sgb[si], pbk[ub]], [hbb[hs]], out=hb[hs][:, fi, 0:n], in0=sg[si][:, 0:n],
                      in1=pall[:, ub, 0:n], op=ALU.mult)
                if pending[0] is not None:
                    pending[0]()
                pending[0] = make_down(s, wdv, nf, hs, off, n)
        pending[0]()

    def dense_add(wslot_s, wv, src, srcbufs, tiles, nk):
        for (off, n) in tiles:
            for dm in range(8):
                ab = 4 + rr("acc")
                for k in range(nk):
                    T("matmul", [wb[wslot_s]] + bl(srcbufs, off, n), [pbk[ab]], out=pall[:, ab, 0:n],
                      lhsT=wv[:, k, dm * 128:(dm + 1) * 128], rhs=src[:, k, off:off + n],
                      start=(k == 0), stop=(k == nk - 1))
                V("tensor_tensor", [pbk[ab]] + bl(xb, off, n), bl(xb, off, n), out=x[:, dm, off:off + n],
                  in0=pall[:, ab, 0:n], in1=x[:, dm, off:off + n], op=ALU.add)

    def load_w_full(dram_w, s, nk):
        wv = wsl[s][:, 0:nk * D].rearrange("p (k d) -> p k d", k=nk)
        DG(f"w{s}", [], [wb[s]], out=wv, in_=dram_w.rearrange("(k p) d -> p k d", p=128))
        return wv

    def ple(layer, grp, prow, vi):
        rmsnorm(grp.tiles, vi)
        for (off, nt) in grp.subs:
            load_tm_to_fm(prow(off, nt), nt, 256, pT, off, [pTb])
        s1 = next_wslot()
        wgv = load_w_full(w_ple_gate[layer], s1, 8)
        s2 = next_wslot()
        wpv = load_w_full(w_ple_proj[layer], s2, 2)
        for (off, n) in grp.tiles:
            for dm in range(8):
                gb = 0 + rr("gb")
                ub = 2 + rr("ub")
                for k in range(8):
                    T("matmul", [wb[s1]] + bl(xnb, off, n), [pbk[gb]], out=pall[:, gb, 0:n],
                      lhsT=wgv[:, k, dm * 128:(dm + 1) * 128], rhs=xn[:, k, off:off + n], start=(k == 0), stop=(k == 7))
                for k in range(2):
                    T("matmul", [wb[s2], pTb], [pbk[ub]], out=pall[:, ub, 0:n],
                      lhsT=wpv[:, k, dm * 128:(dm + 1) * 128], rhs=pT[:, k, off:off + n], start=(k == 0), stop=(k == 1))
                ti = rr("tf")
                A("activation", [pbk[gb]], [tfb[ti]], out=tf32[ti][:, 0:n], in_=pall[:, gb, 0:n], func=AF.Sigmoid)
                V("tensor_tensor", [tfb[ti], pbk[ub]], [tfb[ti]], out=tf32[ti][:, 0:n], in0=tf32[ti][:, 0:n],
                  in1=pall[:, ub, 0:n], op=ALU.mult)
                V("tensor_tensor", [tfb[ti]] + bl(xb, off, n), bl(xb, off, n), out=x[:, dm, off:off + n],
                  in0=tf32[ti][:, 0:n], in1=x[:, dm, off:off + n], op=ALU.add)

    def fm_to_dram(src3, ncols, dram_rows, srcbufs):
        s = rr("stg")
        for half in range(2):
            bk = 6 + rr("tp")
            for c in range(4):
                n = half * 4 + c
                T("transpose", srcbufs + [cb], [pbk[bk]], out=pall[0:ncols, bk, c * 128:(c + 1) * 128],
                  in_=src3[:, n, 0:ncols], identity=identf[:])
            if half == 0:
                A("activation", [pbk[bk]], [stgb[s]], out=stg[s][0:ncols, 0:512], in_=pall[0:ncols, bk, :], func=AF.Copy)
            else:
                V("tensor_copy", [pbk[bk]], [stgb[s]], out=stg[s][0:ncols, 512:1024], in_=pall[0:ncols, bk, :])
        DS(f"stg{s}", [stgb[s]], [], out=dram_rows, in_=stg[s][0:ncols, :])

    def mixer0(grp, si, g):
        T_ = grp.T
        is_s = grp.kind == "s"
        tiles = grp.tiles
        mk = P.new_phase()
        cv_ = Carve()
        TW = T_ + 8
        gg = [cv_.f32(T_) for _ in range(2)]
        ue = [cv_.f32(TW) for _ in range(2)]
        conv = cv_.f32(T_)
        tA = cv_.f32(T_)
        tI = cv_.f32(T_)
        tB = cv_.f32(T_)
        conv_bf = cv_.bf16(T_)
        ggb = [mk("gg0"), mk("gg1")]
        ueb = [mk("ue0"), mk("ue1")]
        convb, tAb, tIb, tBb, cbfb = mk("conv"), mk("tA"), mk("tI"), mk("tB"), mk("cbf")
        if is_s:
            ues = [cv_.f32(NSS * 7).rearrange("p (s k) -> p s k", s=NSS) for _ in range(2)]
            sconvT = cv_.f32(8 * NSS * 3).rearrange("p (n c) -> p n c", n=8)
            shT = cv_.f32(8 * NSS).rearrange("p (n c) -> p n c", n=8)
            nhsT = cv_.f32(8 * NSS).rearrange("p (n c) -> p n c", n=8)
            ncsT = cv_.f32(8 * NSS * 3).rearrange("p (n c) -> p n c", n=8)
            uesb = [mk("ues0"), mk("ues1")]
            stb = mk("stT")
            outb = mk("outT")
            for r0 in range(0, NSS * 3, 96):
                nr = min(96, NSS * 3 - r0)
                load_tm_to_fm(sconv[r0:r0 + nr, :], nr, D, sconvT, r0, [stb])
            for r0 in range(0, NSS, 128):
                nr = min(128, NSS - r0)
                load_tm_to_fm(sh[r0:r0 + nr, :], nr, D, shT, r0, [stb])
        elif g == 0:
            V("memset", [], [hcb], ap=hcarry[:], constant=0.0)
            V("memset", [], [ucb], ap=ucarry[:], constant=0.0)
        slots = {}
        win = w_rg_in[0].rearrange("(kc p) f -> p kc f", p=128)

        def load_in_unit(u):
            s = next_wslot()
            gv = wsl[s][:, 0:4096].rearrange("p (k f) -> p k f", k=8)
            uv = wsl[s][:, 4096:8192].rearrange("p (k f) -> p k f", k=8)
            DG(f"w{s}", [], [wb[s]], out=gv, in_=win[:, :, u * 512:(u + 1) * 512])
            DG(f"w{s}", [], [wb[s]], out=uv, in_=win[:, :, D + u * 512:D + (u + 1) * 512])
            slots[u] = (s, gv, uv)

        load_in_unit(0)
        load_in_unit(1)
        for n in range(8):
            su = n % 2
            s, gv, uv = slots[n // 4]
            c0 = (n % 4) * 128
            for (off, nn) in tiles:
                gb = 0 + rr("gb")
                ub = 2 + rr("ub")
                for k in range(8):
                    T("matmul", [wb[s]] + bl(xnb, off, nn), [pbk[gb]], out=pall[:, gb, 0:nn],
                      lhsT=gv[:, k, c0:c0 + 128], rhs=xn[:, k, off:off + nn], start=(k == 0), stop=(k == 7))
                for k in range(8):
                    T("matmul", [wb[s]] + bl(xnb, off, nn), [pbk[ub]], out=pall[:, ub, 0:nn],
                      lhsT=uv[:, k, c0:c0 + 128], rhs=xn[:, k, off:off + nn], start=(k == 0), stop=(k == 7))
                A("activation", [pbk[gb]], [ggb[su]], out=gg[su][:, off:off + nn], in_=pall[:, gb, 0:nn],
                  func=AF.Gelu_apprx_tanh)
                if not is_s:
                    A("activation", [pbk[ub]], [ueb[su]], out=ue[su][:, 3 + off:3 + off + nn], in_=pall[:, ub, 0:nn],
                      func=AF.Copy)
                else:
                    ns_ = nn // 4
                    s0 = off // 4
                    A("activation", [pbk[ub]], [uesb[su]], out=ues[su][:, s0:s0 + ns_, 3:7],
                      in_=pall[:, ub, 0:nn].rearrange("p (s t) -> p s t", t=4), func=AF.Copy)
            if not is_s:
                V("tensor_copy", [ucb], [ueb[su]], out=ue[su][:, 0:3], in_=ucarry[:, n, :])
                V("tensor_scalar", [ueb[su], cb], [convb], out=conv[:, 0:T_], in0=ue[su][:, 0:T_],
                  scalar1=vcol(V_CW0, n), scalar2=vcol(V_CB, n), op0=ALU.mult, op1=ALU.add)
                for k in range(1, 4):
                    V("scalar_tensor_tensor", [ueb[su], cb, convb], [convb], out=conv[:, 0:T_], in0=ue[su][:, k:k + T_],
                      scalar=vcol(V_CW0 + k, n), in1=conv[:, 0:T_], op0=ALU.mult, op1=ALU.add)
                V("tensor_copy", [ueb[su]], [ucb], out=ucarry[:, n, :], in_=ue[su][:, T_:T_ + 3])
            else:
                cs = conv[:, 0:T_].rearrange("p (s t) -> p s t", t=4)
                V("tensor_copy", [stb], [uesb[su]], out=ues[su][:, :, 0:3],
                  in_=sconvT[:, n, :].rearrange("p (s k) -> p s k", k=3))
                V("tensor_scalar", [uesb[su], cb], [convb], out=cs, in0=ues[su][:, :, 0:4],
                  scalar1=vcol(V_CW0, n), scalar2=vcol(V_CB, n), op0=ALU.mult, op1=ALU.add)
                for k in range(1, 4):
                    V("scalar_tensor_tensor", [uesb[su], cb, convb], [convb], out=cs, in0=ues[su][:, :, k:k + 4],
                      scalar=vcol(V_CW0 + k, n), in1=cs, op0=ALU.mult, op1=ALU.add)
                V("tensor_copy", [uesb[su]], [outb], out=ncsT[:, n, :].rearrange("p (s k) -> p s k", k=3),
                  in_=ues[su][:, :, 4:7])
            A("activation", [convb], [cbfb], out=conv_bf[:, 0:T_], in_=conv[:, 0:T_], func=AF.Copy)
            for (off, nn) in tiles:
                gb = 0 + rr("gb")
                ub = 2 + rr("ub")
                T("matmul", [wgb, cbfb], [pbk[gb]], out=pall[:, gb, 0:nn], lhsT=wg[:, 0, n, :],
                  rhs=conv_bf[:, off:off + nn], start=True, stop=True)
                T("matmul", [wgb, cbfb], [pbk[ub]], out=pall[:, ub, 0:nn], lhsT=wg[:, 1, n, :],
                  rhs=conv_bf[:, off:off + nn], start=True, stop=True)
                A("activation", [pbk[gb], cb], [tAb], out=tA[:, off:off + nn], in_=pall[:, gb, 0:nn], func=AF.Sigmoid,
                  bias=vcol(V_BA, n))
                A("activation", [pbk[ub], cb], [tIb], out=tI[:, off:off + nn], in_=pall[:, ub, 0:nn], func=AF.Sigmoid,
                  bias=vcol(V_BI, n))
            A("activation", [tAb, cb], [tAb], out=tA[:, 0:T_], in_=tA[:, 0:T_], func=AF.Exp, scale=spneg8[:, n:n + 1])
            A("activation", [tAb], [tBb], out=tB[:, 0:T_], in_=tA[:, 0:T_], func=AF.Square)
            A("activation", [tBb], [tBb], out=tB[:, 0:T_], in_=tB[:, 0:T_], func=AF.Sqrt, scale=-1.0, bias=1.0)
            V("tensor_tensor", [tBb, tIb], [tIb], out=tI[:, 0:T_], in0=tB[:, 0:T_], in1=tI[:, 0:T_], op=ALU.mult)
            V("tensor_tensor", [tIb, convb], [tBb], out=tB[:, 0:T_], in0=tI[:, 0:T_], in1=conv[:, 0:T_], op=ALU.mult)
            if not is_s:
                V("tensor_tensor_scan", [tAb, tBb, hcb, convb], [convb], out=conv[:, 0:T_], data0=tA[:, 0:T_],
                  data1=tB[:, 0:T_], initial=hcarry[:, n:n + 1], op0=ALU.mult, op1=ALU.add)
                V("tensor_copy", [convb], [hcb], out=hcarry[:, n:n + 1], in_=conv[:, T_ - 1:T_])
            else:
                hs_ = conv[:, 0:T_].rearrange("p (s t) -> p s t", t=4)
                as_ = tA[:, 0:T_].rearrange("p (s t) -> p s t", t=4)
                bs_ = tB[:, 0:T_].rearrange("p (s t) -> p s t", t=4)
                prev = shT[:, n, :]
                for t in range(4):
                    V("tensor_tensor", [tAb, stb, convb], [convb], out=hs_[:, :, t], in0=as_[:, :, t], in1=prev, op=ALU.mult)
                    V("tensor_tensor", [tBb, convb], [convb], out=hs_[:, :, t], in0=hs_[:, :, t], in1=bs_[:, :, t], op=ALU.add)
                    prev = hs_[:, :, t]
                V("tensor_copy", [convb], [outb], out=nhsT[:, n, :], in_=hs_[:, :, 3])
            V("tensor_tensor", [ggb[su], convb], bl(mob, 0, T_), out=mixo[:, n, 0:T_], in0=gg[su][:, 0:T_],
              in1=conv[:, 0:T_], op=ALU.mult)
        s = next_wslot()
        wov = load_w_full(w_rg_out[0], s, 8)
        dense_add(s, wov, mixo, mob, tiles, 8)
        if not is_s and g == 3:
            fm_to_dram(ucarry[:], 3, nconv_p[si * 3:si * 3 + 3, :], [ucb])
            fm_to_dram(hcarry[:].rearrange("p (n o) -> p n o", o=1), 1, nh_p[si:si + 1, :], [hcb])
        if is_s:
            for r0 in range(0, NSS * 3, 96):
                nr = min(96, NSS * 3 - r0)
                fm_to_dram(ncsT[:, :, r0:r0 + nr], nr, nconv_s[r0:r0 + nr, :], [outb])
            for r0 in range(0, NSS, 128):
                nr = min(128, NSS - r0)
                fm_to_dram(nhsT[:, :, r0:r0 + nr], nr, nh_s[r0:r0 + nr, :], [outb])

    def rope_tm(src_ap, ntok, cos_ap, sin_ap, out4, tmp, srcbufs, outbufs, tmpb):
        sv = src_ap.rearrange("p (h c d) -> p h c d", h=16, c=2)
        cosb = cos_ap.unsqueeze(1).to_broadcast([ntok, 16, 32])
        sinb = sin_ap.unsqueeze(1).to_broadcast([ntok, 16, 32])
        V("tensor_tensor", srcbufs + [cb], outbufs, out=out4[:, :, 0, :], in0=sv[:, :, 0, :], in1=cosb, op=ALU.mult)
        V("tensor_tensor", srcbufs + [cb], [tmpb], out=tmp, in0=sv[:, :, 1, :], in1=sinb, op=ALU.mult)
        V("tensor_tensor", outbufs + [tmpb], outbufs, out=out4[:, :, 0, :], in0=out4[:, :, 0, :], in1=tmp, op=ALU.subtract)
        V("tensor_tensor", srcbufs + [cb], outbufs, out=out4[:, :, 1, :], in0=sv[:, :, 1, :], in1=cosb, op=ALU.mult)
        V("tensor_tensor", srcbufs + [cb, tmpb], [tmpb], out=tmp, in0=sv[:, :, 0, :], in1=sinb, op=ALU.mult)
        V("tensor_tensor", outbufs + [tmpb], outbufs, out=out4[:, :, 1, :], in0=out4[:, :, 1, :], in1=tmp, op=ALU.add)

    def kv_phase(grp, si, g):
        is_s = grp.kind == "s"
        rmsnorm(grp.tiles, V_KVN)
        mk = P.new_phase()
        cv_ = Carve()
        kst = [cv_.f32(D) for _ in range(2)]
        vst = [cv_.f32(D) for _ in range(2)]
        kbf = [cv_.bf16(D) for _ in range(2)]
        ktst = [cv_.bf16(D) for _ in range(2)]
        vbf = [cv_.bf16(8 * 129).rearrange("p (h v) -> p h v", h=8) for _ in range(2)]
        rtc = cv_.f32(256).rearrange("p (t d) -> p t d", t=8)
        rts = cv_.f32(256).rearrange("p (t d) -> p t d", t=8)
        tmpr = [cv_.f32(512).rearrange("p (h d) -> p h d", h=16) for _ in range(2)]
        kstb = [mk(), mk()]
        vstb = [mk(), mk()]
        kbfb = [mk(), mk()]
        ktstb = [mk(), mk()]
        vbfb = [mk(), mk()]
        rtb = mk()
        tmpb = [mk(), mk()]
        if not is_s:
            DS("rt", [], [rtb], out=rtc, in_=c_ropek_c[:, g * 8:(g + 1) * 8, :])
            DS("rt", [], [rtb], out=rts, in_=c_ropek_s[:, g * 8:(g + 1) * 8, :])
            for i in range(2):
                V("memset", [], [vbfb[i]], ap=vbf[i][:, :, 128:129], constant=1.0)
        sK = next_wslot()
        wkv_v = w_kv.rearrange("(k p) f -> p k f", p=128)
        wK = wsl[sK][:, 0:8 * D].rearrange("p (k d) -> p k d", k=8)
        DG(f"w{sK}", [], [wb[sK]], out=wK, in_=wkv_v[:, :, 0:D])
        sV = next_wslot()
        wV = wsl[sV][:, 0:8 * D].rearrange("p (k d) -> p k d", k=8)
        DG(f"w{sV}", [], [wb[sV]], out=wV, in_=wkv_v[:, :, D:2 * D])
        for (off, nt) in grp.subs:
            gt = g * 8 + off // 128
            r = rr("kvr")
            for half in range(2):
                for k in range(8):
                    T("matmul", [wb[sK]] + bl(xnb, off, nt), [pbk[half]], out=pall[0:nt, half, :],
                      lhsT=xn[:, k, off:off + nt], rhs=wK[:, k, half * 512:(half + 1) * 512], start=(k == 0), stop=(k == 7))
            for half in range(2):
                for k in range(8):
                    T("matmul", [wb[sV]] + bl(xnb, off, nt), [pbk[2 + half]], out=pall[0:nt, 2 + half, :],
                      lhsT=xn[:, k, off:off + nt], rhs=wV[:, k, half * 512:(half + 1) * 512], start=(k == 0), stop=(k == 7))
            ksrc = pall[0:nt, 0:2, :].rearrange("p b f -> p (b f)")
            vsrc = pall[0:nt, 2:4, :].rearrange("p b f -> p (b f)")
            if is_s:
                cos_ap, sin_ap = ropes_c[0:nt, :], ropes_s[0:nt, :]
                rbufs = [pbk[0], pbk[1]]
            else:
                cos_ap, sin_ap = rtc[:, off // 128, :], rts[:, off // 128, :]
                rbufs = [pbk[0], pbk[1], rtb]
            out4 = kst[r][0:nt, :].rearrange("p (h c d) -> p h c d", h=16, c=2)
            rope_tm(ksrc, nt, cos_ap, sin_ap, out4, tmpr[r][0:nt], rbufs, [kstb[r]], tmpb[r])
            A("activation", [pbk[2], pbk[3]], [vstb[r]], out=vst[r][0:nt, :], in_=vsrc, func=AF.Copy)
            if is_s:
                DS(f"kst{r}", [kstb[r]], [], out=nk_s[off:off + nt, :], in_=kst[r][0:nt, :])
                DS(f"vst{r}", [vstb[r]], [], out=nv_s[off:off + nt, :], in_=vst[r][0:nt, :])
            else:
                row0 = si * SEQ + gt * 128
                DS(f"kst{r}", [kstb[r]], [], out=nk_p[row0:row0 + 128, :], in_=kst[r][0:nt, :])
                DS(f"vst{r}", [vstb[r]], [], out=nv_p[row0:row0 + 128, :], in_=vst[r][0:nt, :])
            A("activation", [kstb[r]], [kbfb[r]], out=kbf[r][0:nt, :], in_=kst[r][0:nt, :], func=AF.Copy)
            bk = 6 + rr("tp")
            pbv = pall[:, bk, :].bitcast(BF16)
            for j in range(8):
                T("transpose", [kbfb[r], cb], [pbk[bk]], out=pbv[:, j * 128:j * 128 + nt],
                  in_=kbf[r][0:nt, j * 128:(j + 1) * 128], identity=identb[0:nt, 0:nt])
            if is_s:
                V("tensor_copy", [pbk[bk]], [ksb], out=KsT[:, :, off:off + nt],
                  in_=pbv.rearrange("p (j t) -> p j t", j=8)[:, :, 0:nt])
                V("tensor_copy", [vstb[r]], [vsb], out=Vs_bf[0:nt, off // 64, :], in_=vst[r][0:nt, :])
            else:
                V("tensor_copy", [pbk[bk]], [ktstb[r]], out=ktst[r], in_=pbv)
                DS(f"ktst{r}", [ktstb[r]], [ktb[gt]], out=ktscr[:, :, gt * 128:(gt + 1) * 128],
                   in_=ktst[r].rearrange("p (j t) -> p j t", j=8))
                V("tensor_copy", [vstb[r]], [vbfb[r]], out=vbf[r][:, :, 0:128],
                  in_=vst[r].rearrange("p (h v) -> p h v", h=8))
                DS(f"vbf{r}", [vbfb[r]], [ktb[gt]], out=vscr[:, gt, :], in_=vbf[r].rearrange("p h v -> p (h v)"))
        if not is_s:
            DS("x1st", bl(xb, 0, GP), [x1b[g]], out=x1scr[:, :, g * GP:(g + 1) * GP], in_=x[:, :, 0:GP])

    def q_phase(grp, g):
        is_s = grp.kind == "s"
        mk = P.new_phase()
        cv_ = Carve()
        qbf = [cv_.bf16(D) for _ in range(2)]
        tmpr = [cv_.f32(512).rearrange("p (h d) -> p h d", h=16) for _ in range(2)]
        rtc = cv_.f32(256).rearrange("p (t d) -> p t d", t=8)
        rts = cv_.f32(256).rearrange("p (t d) -> p t d", t=8)
        qbfb = [mk(), mk()]
        tmpb = [mk(), mk()]
        rtb = mk()
        if not is_s:
            DS("rt", [], [rtb], out=rtc, in_=c_ropek_c[:, g * 8:(g + 1) * 8, :])
            DS("rt", [], [rtb], out=rts, in_=c_ropek_s[:, g * 8:(g + 1) * 8, :])
        s = next_wslot()
        wq = load_w_full(w_q[0], s, 8)
        for (off, nt) in grp.subs:
            r = rr("qr")
            for half in range(2):
                for k in range(8):
                    T("matmul", [wb[s]] + bl(xnb, off, nt), [pbk[half]], out=pall[0:nt, half, :],
                      lhsT=xn[:, k, off:off + nt], rhs=wq[:, k, half * 512:(half + 1) * 512], start=(k == 0), stop=(k == 7))
            qsrc = pall[0:nt, 0:2, :].rearrange("p b f -> p (b f)")
            if is_s:
                cos_ap, sin_ap = ropes_c[0:nt, :], ropes_s[0:nt, :]
                rbufs = [pbk[0], pbk[1]]
            else:
                cos_ap, sin_ap = rtc[:, off // 128, :], rts[:, off // 128, :]
                rbufs = [pbk[0], pbk[1], rtb]
            out4 = qbf[r][0:nt, :].rearrange("p (h c d) -> p h c d", h=16, c=2)
            rope_tm(qsrc, nt, cos_ap, sin_ap, out4, tmpr[r][0:nt], rbufs, [qbfb[r]], tmpb[r])
            bk = 6 + rr("tp")
            pbv = pall[:, bk, :].bitcast(BF16)
            for j in range(8):
                T("transpose", [qbfb[r], cb], [pbk[bk]], out=pbv[:, j * 128:j * 128 + nt],
                  in_=qbf[r][0:nt, j * 128:(j + 1) * 128], identity=identb[0:nt, 0:nt])
            A("activation", [pbk[bk]], bl(xnb, off, nt), out=xn[:, :, off:off + nt],
              in_=pbv.rearrange("p (j t) -> p j t", j=8)[:, :, 0:nt], func=AF.Copy)

    def attention(g):
        mk = P.new_phase()
        cv_ = Carve()
        nkb = 8 * g + 8
        KTh = [cv_.bf16(SEQ) for _ in range(2)]
        Vh = [cv_.bf16(32 * 129).rearrange("p (b v) -> p b v", b=32) for _ in range(2)]
        PT = [[cv_.bf16(512) for _ in range(2)] for _ in range(2)]
        t1 = cv_.f32(128)
        dd = cv_.f32(128)
        dn = cv_.bf16(128)
        junk = cv_.bf16(128)
        sc = cv_.f32(8)
        kthb = [mk(), mk()]
        vhb = [mk(), mk()]
        ptb = [[mk(), mk()], [mk(), mk()]]
        epb = mk()
        ob = [[pbk[4 + (j * 2 + c) // 3] for c in range(2)] for j in range(4)]

        def oreg(j, c):
            r = j * 2 + c
            return pall[:, 4 + r // 3, (r % 3) * 129:(r % 3) * 129 + 129]

        ocp = sq[:].rearrange("p a b -> p (a b)").bitcast(F32)

        def oreg_sb(j, c):
            r = j * 2 + c
            return ocp[:, (r // 3) * 512 + (r % 3) * 129:(r // 3) * 512 + (r % 3) * 129 + 129]

        for h in range(8):
            s = h % 2
            DS(f"kth{s}", ktb[0:nkb], [kthb[s]], out=KTh[s][:, 0:nkb * 128], in_=ktscr[:, h, 0:nkb * 128])
            DS(f"vh{s}", ktb[0:nkb], [vhb[s]], out=Vh[s][:, 0:nkb, :], in_=vscr[:, 0:nkb, h * 129:(h + 1) * 129])
            for sgi in range(2):
                i0 = 8 * g + 4 * sgi
                cols0 = 4 * sgi * 128
                nkbs = i0 + 4
                opened = set()

                def emit_qk(kb):
                    j0 = max(0, kb - i0)
                    diag = kb - i0 if kb >= i0 else None
                    alt = rr("salt")
                    for c in range(2):
                        bk = 2 * c + alt
                        ps = pall[:, bk, :]
                        pr = slice(64 * c, 64 * c + 64)
                        jr = j0
                        if diag is not None:
                            j = diag
                            T("matmul", [cb], [pbk[bk]], out=ps[:, j * 128:(j + 1) * 128], lhsT=identb[:], rhs=tri[:],
                              start=True, stop=False)
                            T("matmul", [kthb[s]] + bl(xnb, cols0 + j * 128, 128), [pbk[bk]],
                              out=ps[:, j * 128:(j + 1) * 128], lhsT=KTh[s][pr, kb * 128:(kb + 1) * 128],
                              rhs=xn[pr, h, cols0 + j * 128:cols0 + (j + 1) * 128], start=False, stop=True)
                            jr = j0 + 1
                        if jr < 4:
                            T("matmul", [kthb[s]] + bl(xnb, cols0 + jr * 128, (4 - jr) * 128), [pbk[bk]],
                              out=ps[:, jr * 128:512], lhsT=KTh[s][pr, kb * 128:(kb + 1) * 128],
                              rhs=xn[pr, h, cols0 + jr * 128:cols0 + 512], start=True, stop=True)
                    return alt, j0

                def emit_exp_pv(kb, alt, j0):
                    for c in range(2):
                        bk = 2 * c + alt
                        A("activation", [pbk[bk]], [ptb[c][alt]], out=PT[c][alt][:, j0 * 128:512],
                          in_=pall[:, bk, j0 * 128:512], func=AF.Exp, scale=SCALE)
                    for c in range(2):
                        for j in range(j0, 4):
                            obank = 4 + (j * 2 + c) // 3
                            st = (kb == 0) and (obank not in opened)
                            opened.add(obank)
                            T("matmul", [ptb[c][alt], vhb[s]], [ob[j][c]], out=oreg(j, c),
                              lhsT=PT[c][alt][:, j * 128:(j + 1) * 128], rhs=Vh[s][:, kb, :],
                              start=st, stop=(kb == i0 + j))

                pend = emit_qk(0)
                for kb in range(nkbs):
                    nxt = emit_qk(kb + 1) if kb + 1 < nkbs else None
                    emit_exp_pv(kb, pend[0], pend[1])
                    pend = nxt
                A("activation", [pbk[4]], [sqb], out=ocp[:, 0:387], in_=pall[:, 4, 0:387], func=AF.Copy)
                V("tensor_copy", [pbk[5]], [sqb], out=ocp[:, 512:512 + 387], in_=pall[:, 5, 0:387])
                A("activation", [pbk[6]], [sqb], out=ocp[:, 1024:1024 + 258], in_=pall[:, 6, 0:258], func=AF.Copy)
                for j in range(4):
                    o0, o1 = oreg_sb(j, 0), oreg_sb(j, 1)
                    col = cols0 + j * 128
                    V("reciprocal", [sqb], [epb], out=sc[:, 0:1], in_=o0[:, 128:129])
                    V("reciprocal", [sqb], [epb], out=sc[:, 1:2], in_=o1[:, 128:129])
                    V("tensor_tensor", [epb, cb], [epb], out=sc[:, 1:2], in0=sc[:, 1:2], in1=neglam[:], op=ALU.mult)
                    V("tensor_scalar", [sqb, epb], [epb], out=t1, in0=o1[:, 0:128], scalar1=sc[:, 1:2],
                      scalar2=None, op0=ALU.mult)
                    V("scalar_tensor_tensor", [sqb, epb], [epb], out=dd, in0=o0[:, 0:128], scalar=sc[:, 0:1],
                      in1=t1, op0=ALU.mult, op1=ALU.add)
                    V("memset", [], [epb], ap=sc[:, 2:3], constant=0.0)
                    A("activation", [epb], [epb], out=junk, in_=dd, func=AF.Square, accum_out=sc[:, 2:3])
                    A("activation", [epb], [epb], out=sc[:, 3:4], in_=sc[:, 2:3], func=AF.Sqrt, scale=1.0 / 128, bias=EPS)
                    V("reciprocal", [epb], [epb], out=sc[:, 4:5], in_=sc[:, 3:4])
                    V("scalar_tensor_tensor", [epb, cb], [epb], out=dn, in0=dd, scalar=sc[:, 4:5], in1=subln_b[:],
                      op0=ALU.mult, op1=ALU.mult)
                    pbv = pall[:, 7, :].bitcast(BF16)
                    T("transpose", [epb, cb], [pbk[7]], out=pbv[:, 0:128], in_=dn, identity=identb[:])
                    A("activation", [pbk[7]], bl(mob, col, 128), out=mixo[:, h, col:col + 128], in_=pbv[:, 0:128],
                      func=AF.Copy)

    def sample_attention():
        mk = P.new_phase()
        cv_ = Carve()
        NK = int(os.environ.get("MK_NK", "3"))
        kpg = [kvpg[i][:] for i in range(NK)]
        vpg = [kvpg[NK + i][:] for i in range(NK)]
        ktpg = [cv_.bf16(D) for _ in range(2)]
        S_sb = cv_.f32(2048 + 64)
        Pb = cv_.bf16(2048 + 64)
        PTs = cv_.bf16(D)
        PTn = cv_.bf16(64)
        qpad = cv_.bf16(576)
        tsel = S_sb[:, 0:D]
        osel = cv_.f32(128)
        mx = cv_.f32(8)
        dsall = cv_.f32(8 * 64)
        smask = cv_.f32(D).rearrange("p (s k) -> p s k", s=16)
        kpgb = [mk() for _ in range(NK)]
        vpgb = [mk() for _ in range(NK)]
        ktpgb = [mk(), mk()]
        ssb, pbb, ptsb, ptnb, qpb, oselb, mxb, dsb, smkb = [mk() for _ in range(9)]
        tselb = ssb
        DS("smask", [], [smkb], out=smask[0:64], in_=c_smask)
        V("memset", [], [qpb], ap=qpad, constant=0.0)
        qpv = qpad.rearrange("p (j c) -> p j c", c=72)
        sa_n = int(os.environ.get("MK_SA_N", str(NSS)))
        sa_nov = bool(os.environ.get("MK_SA_NOV"))
        for s in range(sa_n):
            w = s // 16
            sl = s % 16
            qc = 4 * s
            V("tensor_copy", bl(xnb, qc, 4), [qpb], out=qpv[0:64, :, 0:4], in_=xn[0:64, :, qc:qc + 4])
            V("tensor_copy", bl(xnb, qc, 4), [qpb], out=qpv[64:128, :, 4:8], in_=xn[64:128, :, qc:qc + 4])
            for pg in range(16):
                ks = rr("kpg", NK)
                P.op("gpsimd", (lambda ks=ks, col=s * 16 + pg: nc.gpsimd.indirect_dma_start(
                    out=kpg[ks], out_offset=None, in_=ck,
                    in_offset=bass.IndirectOffsetOnAxis(ap=idxall[:, col:col + 1], axis=0))),
                    [cb], [kpgb[ks]], dma=f"kpg{ks}")
                bk = 6 + rr("tp")
                pbv = pall[:, bk, :].bitcast(BF16)
                for j in range(8):
                    T("transpose", [kpgb[ks], cb], [pbk[bk]], out=pbv[:, j * 128:(j + 1) * 128],
                      in_=kpg[ks][:, j * 128:(j + 1) * 128], identity=identb[:])
                ka = rr("ktpg")
                if pg % 2 == 0:
                    A("activation", [pbk[bk]], [ktpgb[ka]], out=ktpg[ka], in_=pbv, func=AF.Copy)
                else:
                    V("tensor_copy", [pbk[bk]], [ktpgb[ka]], out=ktpg[ka], in_=pbv)
                quad = pg // 4
                sbk = 0 + (quad % 2)
                for j in range(8):
                    T("matmul", [qpb, ktpgb[ka]], [pbk[sbk]], out=pall[0:64, sbk, (pg % 4) * 128:(pg % 4 + 1) * 128],
                      lhsT=qpad[:, j * 64:(j + 1) * 64], rhs=ktpg[ka][:, j * 128:(j + 1) * 128],
                      start=(j == 0), stop=(j == 7))
                if pg % 4 == 3:
                    V("tensor_reduce", [pbk[sbk]], [mxb], out=mx[0:64, quad:quad + 1], in_=pall[0:64, sbk, :],
                      axis=AX.X, op=ALU.max)
                    A("activation", [pbk[sbk]], [ssb], out=S_sb[0:64, quad * 512:(quad + 1) * 512],
                      in_=pall[0:64, sbk, :], func=AF.Copy)
            if sa_nov:
                continue
            for pg in range(16):
                vs_ = rr("vpg", NK)
                P.op("gpsimd", (lambda vs_=vs_, col=s * 16 + pg: nc.gpsimd.indirect_dma_start(
                    out=vpg[vs_], out_offset=None, in_=cv,
                    in_offset=bass.IndirectOffsetOnAxis(ap=idxall[:, col:col + 1], axis=0))),
                    [cb], [vpgb[vs_]], dma=f"vpg{vs_}")
                if pg == 0:
                    for j in range(8):
                        T("matmul", [qpb, ksb], [pbk[2]], out=pall[0:64, 2, 0:64], lhsT=qpad[:, j * 64:(j + 1) * 64],
                          rhs=KsT[:, j, 64 * w:64 * w + 64], start=(j == 0), stop=(j == 7))
                    V("tensor_tensor", [pbk[2], smkb], [ssb], out=S_sb[0:64, 2048:2112], in0=pall[0:64, 2, 0:64],
                      in1=smask[0:64, sl, :], op=ALU.add)
                    V("tensor_reduce", [ssb], [mxb], out=mx[0:64, 4:5], in_=S_sb[0:64, 2048:2112], axis=AX.X, op=ALU.max)
                    V("tensor_reduce", [mxb], [mxb], out=mx[0:64, 5:6], in_=mx[0:64, 0:5], axis=AX.X, op=ALU.max)
                    V("tensor_scalar", [mxb], [mxb], out=mx[0:64, 6:7], in0=mx[0:64, 5:6], scalar1=-SCALE, scalar2=None,
                      op0=ALU.mult)
                    V("memset", [], [mxb], ap=mx[0:64, 7:8], constant=0.0)
                    A("activation", [ssb, mxb], [pbb, mxb], out=Pb[0:64, :], in_=S_sb[0:64, :], func=AF.Exp, scale=SCALE,
                      bias=mx[0:64, 6:7], accum_out=mx[0:64, 7:8])
                    pbv = pall[:, 3, :].bitcast(BF16)
                    for b in range(16):
                        T("transpose", [pbb, cb], [pbk[3]], out=pbv[:, b * 64:(b + 1) * 64], in_=Pb[0:64, b * 128:(b + 1) * 128],
                          identity=identb[0:64, 0:64])
                    pbn = pall[:, 2, :].bitcast(BF16)
                    T("transpose", [pbb, cb], [pbk[2]], out=pbn[0:64, 512:576], in_=Pb[0:64, 2048:2112],
                      identity=identb[0:64, 0:64])
                    A("activation", [pbk[3]], [ptsb], out=PTs, in_=pbv, func=AF.Copy)
                    V("tensor_copy", [pbk[2]], [ptnb], out=PTn[0:64, :], in_=pbn[0:64, 512:576])
                for half in range(2):
                    T("matmul", [ptsb, vpgb[vs_]], [pbk[4 + half]], out=pall[0:64, 4 + half, :],
                      lhsT=PTs[:, pg * 64:(pg + 1) * 64], rhs=vpg[vs_][:, half * 512:(half + 1) * 512],
                      start=(pg == 0), stop=False)
            for half in range(2):
                T("matmul", [ptnb, vsb], [pbk[4 + half]], out=pall[0:64, 4 + half, :], lhsT=PTn[0:64, :],
                  rhs=Vs_bf[0:64, w, half * 512:(half + 1) * 512], start=False, stop=True)
            osrc = pall[0:64, 4:6, :].rearrange("p b (h v) -> p (b h) v", v=128)
            V("tensor_tensor", [pbk[4], pbk[5], cb], [tselb], out=tsel[0:64].rearrange("p (h v) -> p h v", h=8), in0=osrc,
              in1=hmask[:].unsqueeze(2).to_broadcast([64, 8, 128]), op=ALU.mult)
            V("tensor_reduce", [tselb], [oselb], out=osel[0:64], in_=tsel[0:64].rearrange("p (h v) -> p v h", h=8),
              axis=AX.X, op=ALU.add)
            V("reciprocal", [mxb], [mxb], out=mx[0:64, 7:8], in_=mx[0:64, 7:8])
            V("tensor_scalar", [oselb, mxb], [oselb], out=osel[0:64], in0=osel[0:64], scalar1=mx[0:64, 7:8], scalar2=None,
              op0=ALU.mult)
            T("transpose", [oselb, cb], [pbk[7]], out=pall[:, 7, 0:64], in_=osel[0:64], identity=identf[0:64, 0:64])
            A("activation", [pbk[7]], [oselb], out=osel[:, 0:64], in_=pall[:, 7, 0:64], func=AF.Copy)
            ot = osel[:, 0:64].rearrange("p (h c q) -> p h c q", h=8, c=2)
            V("scalar_tensor_tensor", [oselb, cb], [dsb], out=dsall.rearrange("p (h t) -> p h t", h=8)[:, :, 4 * sl:4 * sl + 4],
              in0=ot[:, :, 1, :], scalar=neglam[:, 0:1], in1=ot[:, :, 0, :], op0=ALU.mult, op1=ALU.add)
            if sl == 15:
                c0 = 64 * w
                A("activation", [dsb], [sqb], out=sq[:, 0, :], in_=dsall, func=AF.Square)
                T("matmul", [sqb, cb], [pbk[6]], out=pall[:, 6, :], lhsT=onesb[:], rhs=sq[:, 0, :], start=True, stop=True)
                A("activation", [pbk[6]], [tfb[0]], out=tf32[0][:], in_=pall[:, 6, :], func=AF.Sqrt, scale=1.0 / 128, bias=EPS)
                V("reciprocal", [tfb[0]], [tfb[0]], out=tf32[0][:], in_=tf32[0][:])
                V("scalar_tensor_tensor", [dsb, tfb[0], cb], bl(mob, c0, 64), out=mixo[:, :, c0:c0 + 64],
                  in0=dsall.rearrange("p (h t) -> p h t", h=8), scalar=subln_p[:, 0:1],
                  in1=tf32[0][:].rearrange("p (h t) -> p h t", h=8), op0=ALU.mult, op1=ALU.mult)

    def final_phase(grp, yrow):
        mk = P.new_phase()
        cv_ = Carve()
        gfin = cv_.f32(D)
        junk = cv_.bf16(D)
        sc = cv_.f32(8)
        gb_, jb, scb = mk(), mk(), mk()
        DS("gfin", [], [gb_], out=gfin, in_=w_final_norm.rearrange("(o d) -> o d", o=1).partition_broadcast(128))
        for (off, nt) in grp.subs:
            for half in range(2):
                for c in range(4):
                    n = half * 4 + c
                    T("transpose", bl(xb, off, nt) + [cb], [pbk[half]], out=pall[0:nt, half, c * 128:(c + 1) * 128],
                      in_=x[:, n, off:off + nt], identity=identf[:])
            src = pall[0:nt, 0:2, :].rearrange("p b f -> p (b f)")
            V("memset", [], [scb], ap=sc[0:nt, 0:1], constant=0.0)
            A("activation", [pbk[0], pbk[1]], [jb, scb], out=junk[0:nt], in_=src, func=AF.Square, accum_out=sc[0:nt, 0:1])
            A("activation", [scb], [scb], out=sc[0:nt, 1:2], in_=sc[0:nt, 0:1], func=AF.Sqrt, scale=1.0 / D, bias=EPS)
            V("reciprocal", [scb], [scb], out=sc[0:nt, 2:3], in_=sc[0:nt, 1:2])
            s = rr("stg")
            V("scalar_tensor_tensor", [pbk[0], pbk[1], scb, gb_], [stgb[s]], out=stg[s][0:nt, :], in0=src,
              scalar=sc[0:nt, 2:3], in1=gfin[0:nt], op0=ALU.mult, op1=ALU.mult)
            DS(f"stg{s}", [stgb[s]], [], out=yrow(off, nt), in_=stg[s][0:nt, :])

    pgrp = Grp("p", GP)
    sgrp = Grp("s", NSMP)

    def layer0(grp, si, g, xrow, prow):
        for (off, nt) in grp.subs:
            load_tm_to_fm(xrow(off, nt), nt, D, x, off, bl(xb, off, nt))
        stage("l0.load")
        ffn(w_ffn1_gu[0], w_ffn1_dn[0], grp.tiles, V_FFN1N0)
        stage("l0.ffn1")
        rmsnorm(grp.tiles, V_MIXN0)
        stage("l0.norm")
        mixer0(grp, si, g)
        stage("l0.mixer")
        ffn(w_ffn2_gu[0], w_ffn2_dn[0], grp.tiles, V_FFN2N0)
        stage("l0.ffn2")
        ple(0, grp, prow, V_PLEN0)
        stage("l0.ple")
        kv_phase(grp, si, g)
        stage("l0.kv")

    def layer1(grp, g, prow, yrow):
        ffn(w_ffn1_gu[1], w_ffn1_dn[1], grp.tiles, V_FFN1N1)
        stage("l1.ffn1")
        rmsnorm(grp.tiles, V_MIXN1)
        q_phase(grp, g)
        stage("l1.q")
        if grp.kind == "p":
            attention(g)
        else:
            sample_attention()
        dbg = os.environ.get("MK_DBG", "")
        if dbg in ("attn", "q") and grp.kind == "p":
            srcT = mixo if dbg == "attn" else xn
            for k in range(8):
                V("tensor_copy", bl(mob, 0, GP) + bl(xnb, 0, GP), bl(xb, 0, GP), out=x[:, k, 0:GP], in_=srcT[:, k, 0:GP])
            for (off, nt) in grp.subs:
                fm_to_dram(x[:, :, off:off + nt], nt, yrow(off, nt), bl(xb, off, nt))
            raise _Stop()
        stage("l1.attn")
        s = next_wslot()
        wov = load_w_full(w_o[0], s, 8)
        dense_add(s, wov, mixo, mob, grp.tiles, 8)
        stage("l1.wo")
        ffn(w_ffn2_gu[1], w_ffn2_dn[1], grp.tiles, V_FFN2N1)
        stage("l1.ffn2")
        ple(1, grp, prow, V_PLEN1)
        stage("l1.ple")
        final_phase(grp, yrow)
        stage("l1.final")

    def program():
        setup()
        stage("setup")
        for si in range(0 if not os.environ.get("MK_SKIPP") else NPS, NPS):
            for g in range(4):
                base = si * SEQ + g * GP
                layer0(pgrp, si, g,
                       (lambda off, nt, base=base: xp[base + off:base + off + nt, :]),
                       (lambda off, nt, base=base: pp0[base + off:base + off + nt, :]))
            for g in range(4):
                base = si * SEQ + g * GP
                DS("x1ld", [x1b[g]], bl(xb, 0, GP), out=x[:, :, 0:GP], in_=x1scr[:, :, g * GP:(g + 1) * GP])
                layer1(pgrp, g,
                       (lambda off, nt, base=base: pp1[base + off:base + off + nt, :]),
                       (lambda off, nt, base=base: y_p[base + off:base + off + nt, :]))
        layer0(sgrp, 0, 0, (lambda off, nt: xs[off:off + nt, :]), (lambda off, nt: ps0[off:off + nt, :]))
        layer1(sgrp, 0, (lambda off, nt: ps1[off:off + nt, :]), (lambda off, nt: y_s[off:off + nt, :]))

    try:
        program()
    except _Stop:
        pass

    P.emit_all()
    P.close()
    return nc


_CACHE = {}


def _rope_tables(pos):
    half = 32
    inv = np.power(np.float32(10000.0), -np.arange(half, dtype=np.float32) * np.float32(2.0) / np.float32(64.0)).astype(np.float32)
    ang = pos.astype(np.float32)[:, None] * inv[None, :]
    return np.cos(ang).astype(np.float32), np.sin(ang).astype(np.float32)


def kernel(**inp):
    f32 = np.float32
    npool = inp["cache_k"].shape[0]
    key = ("nc", npool)
    if key not in _CACHE:
        _CACHE[key] = build_program(npool)
    nc = _CACHE[key]

    ck = np.ascontiguousarray(inp["cache_k"], dtype=f32).reshape(npool * 128, D)
    cv = np.ascontiguousarray(inp["cache_v"], dtype=f32).reshape(npool * 128, D)
    weights = ["ffn1_norm", "ffn1_w_gu", "ffn1_w_down", "mix_norm", "rg_w_in", "rg_conv_w", "rg_conv_b", "rg_w_a",
               "rg_b_a", "rg_w_i", "rg_b_i", "rg_lambda", "rg_w_out", "kv_norm", "w_kv", "attn_w_q", "lambda_q1",
               "lambda_k1", "lambda_q2", "lambda_k2", "attn_subln", "attn_w_o", "ffn2_norm", "ffn2_w_gu",
               "ffn2_w_down", "ple_norm", "ple_w_gate", "ple_w_proj", "final_norm"]
    wd = {k: np.ascontiguousarray(inp[k], dtype=f32) for k in weights}

    cosk, sink = _rope_tables(np.arange(SEQ))
    ropek_c = np.ascontiguousarray(cosk.reshape(32, 128, 32).transpose(1, 0, 2))
    ropek_s = np.ascontiguousarray(sink.reshape(32, 128, 32).transpose(1, 0, 2))
    coss, sins = _rope_tables(2048 + (np.arange(128) % 4))
    tri = np.where(np.arange(128)[:, None] <= np.arange(128)[None, :], 0.0, NEG).astype(f32)
    smask = np.full((64, 16, 64), NEG, f32)
    for s in range(16):
        for r in range(64):
            q = r % 4
            for k in range(q + 1):
                smask[r, s, 4 * s + k] = 0.0
    hmask = np.zeros((64, 8), f32)
    for r in range(64):
        hmask[r, r // 8] = 1.0

    xp_all = np.ascontiguousarray(inp["x_prompt"], dtype=f32)
    xs_all = np.ascontiguousarray(inp["x_sample"], dtype=f32)
    pp_all = np.ascontiguousarray(inp["p_prompt"], dtype=f32)
    ps_all = np.ascontiguousarray(inp["p_sample"], dtype=f32)
    pt_all = np.ascontiguousarray(inp["page_table"], dtype=np.int32)
    sc_all = np.ascontiguousarray(inp["state_conv"], dtype=f32)
    sh_all = np.ascontiguousarray(inp["state_rglru"], dtype=f32)

    in_maps = []
    for c in range(NCORES):
        ps_ = slice(c * NPS, (c + 1) * NPS)
        ss_ = slice(c * NSS, (c + 1) * NSS)
        m = dict(wd)
        m.update({
            "xp": xp_all[ps_].reshape(NPS * SEQ, D), "xs": xs_all[ss_].reshape(NSMP, D),
            "pp0": pp_all[0, ps_].reshape(NPS * SEQ, 256), "pp1": pp_all[1, ps_].reshape(NPS * SEQ, 256),
            "ps0": ps_all[0, ss_].reshape(NSMP, 256), "ps1": ps_all[1, ss_].reshape(NSMP, 256),
            "ck": ck, "cv": cv, "pt": pt_all[ss_].reshape(1, NSS * 16),
            "sconv": sc_all[0, ss_].reshape(NSS * 3, D), "sh": sh_all[0, ss_],
            "ropek_c": ropek_c, "ropek_s": ropek_s, "ropes_c": coss, "ropes_s": sins,
            "tri": tri, "smask": smask, "hmask": hmask,
        })
        in_maps.append(m)

    res = run_bass_kernel_spmd(nc, in_maps, core_ids=list(range(NCORES)))
    R = res.results

    def cat(name):
        return np.concatenate([R[c][name] for c in range(NCORES)], axis=0)

    y_prompt = cat("y_p").reshape(4, SEQ, D)
    y_sample = cat("y_s").reshape(128, 4, D)
    nk_p = cat("nk_p").reshape(4, SEQ, 16, 64)
    nv_p = cat("nv_p").reshape(4, SEQ, 8, 128)
    nk_s = cat("nk_s").reshape(128, 4, 16, 64)
    nv_s = cat("nv_s").reshape(128, 4, 8, 128)
    nconv_p = cat("nconv_p").reshape(1, 4, 3, D)
    nh_p = cat("nh_p").reshape(1, 4, D)
    nconv_s = cat("nconv_s").reshape(1, 128, 3, D)
    nh_s = cat("nh_s").reshape(1, 128, D)
    return (y_prompt, y_sample, nk_p, nv_p, nk_s, nv_s, nconv_p, nh_p, nconv_s, nh_s)
```
